# Optimizing a Trainium2 kernel written in Bass

```python
import math
import jax, jax.numpy as jnp
from jax import lax
import numpy as np

D_MODEL = 1024
BATCH = 4
SEQ = 8192
DEPTH = 4

N_MIXERS = 3
HEAD_DIM = 64
ROT_DIM = HEAD_DIM // 4
ROPE_THETA = 500000.0
NORM_EPS = 1e-6
D_FF = 4 * D_MODEL

DIFF_HEADS = D_MODEL // (2 * HEAD_DIM)
DIFF_QK_WIDTH = DIFF_HEADS * 2 * HEAD_DIM
DIFF_V_DIM = 2 * HEAD_DIM
DIFF_IN = 2 * DIFF_QK_WIDTH + DIFF_HEADS * DIFF_V_DIM
DENSE_Q_BLOCK = 128

MOBA_HEADS = D_MODEL // HEAD_DIM
MOBA_BLOCK = 256
MOBA_TOPK = 3
MOBA_Q_CHUNK = 32
MOBA_IN = 3 * MOBA_HEADS * HEAD_DIM

SWA_HEADS = D_MODEL // HEAD_DIM
SWA_KV_HEADS = SWA_HEADS // 8
SWA_WINDOW = 128
SWA_Q_BLOCK = SWA_WINDOW
SWA_IN = (SWA_HEADS + 2 * SWA_KV_HEADS) * HEAD_DIM

N_DIFF = (DEPTH + 2) // 3
N_MOBA = (DEPTH + 1) // 3
N_SWA = DEPTH // 3

kernel_name = "hybrid_diff_moba_swa_sink_trunk"


def rms_norm(x, g):
    xf = x.astype(jnp.float32)
    y = xf * lax.rsqrt(jnp.mean(xf * xf, axis=-1, keepdims=True) + NORM_EPS)
    return (y * g.astype(jnp.float32)).astype(x.dtype)


def rope_tables(positions):
    inv = ROPE_THETA ** (-jnp.arange(0, ROT_DIM, 2, dtype=jnp.float32) / ROT_DIM)
    ang = positions.astype(jnp.float32)[..., None] * inv
    return jnp.cos(ang), jnp.sin(ang)


def apply_partial_rope(x, cos, sin):
    half = ROT_DIM // 2
    c = cos[:, :, None, :].astype(x.dtype)
    s = sin[:, :, None, :].astype(x.dtype)
    x1 = x[..., :half]
    x2 = x[..., half:ROT_DIM]
    return jnp.concatenate([x1 * c - x2 * s, x2 * c + x1 * s, x[..., ROT_DIM:]], axis=-1)


def diff_attention(h, w_in, w_out, lam_q1, lam_k1, lam_q2, lam_k2, sub_g, cos, sin, lambda_init):
    B, S, _ = h.shape
    H = DIFF_HEADS
    qkv = h @ w_in
    q, k, v = jnp.split(qkv, [DIFF_QK_WIDTH, 2 * DIFF_QK_WIDTH], axis=-1)
    q = apply_partial_rope(q.reshape(B, S, 2 * H, HEAD_DIM), cos, sin) * (HEAD_DIM ** -0.5)
    k = apply_partial_rope(k.reshape(B, S, 2 * H, HEAD_DIM), cos, sin)
    q = q.reshape(B, S, H, 2, HEAD_DIM)
    k = k.reshape(B, S, H, 2, HEAD_DIM)
    v = v.reshape(B, S, H, DIFF_V_DIM)
    f32 = jnp.float32
    lam = (jnp.exp(jnp.sum(lam_q1.astype(f32) * lam_k1.astype(f32)))
           - jnp.exp(jnp.sum(lam_q2.astype(f32) * lam_k2.astype(f32))) + lambda_init)
    nqb = S // DENSE_Q_BLOCK
    qb = q.reshape(B, nqb, DENSE_Q_BLOCK, H, 2, HEAD_DIM).transpose(1, 0, 2, 3, 4, 5)
    key_pos = jnp.arange(S)

    def block(args):
        qi, blk = args
        q_pos = blk * DENSE_Q_BLOCK + jnp.arange(DENSE_Q_BLOCK)
        s = jnp.einsum('bqhcd,bkhcd->bhcqk', qi, k).astype(f32)
        causal = key_pos[None, :] <= q_pos[:, None]
        s = jnp.where(causal, s, -jnp.inf)
        p = jax.nn.softmax(s, axis=-1)
        p = p[:, :, 0] - lam * p[:, :, 1]
        return jnp.einsum('bhqk,bkhe->bqhe', p.astype(v.dtype), v)

    o = lax.map(block, (qb, jnp.arange(nqb)))
    o = o.transpose(1, 0, 2, 3, 4).reshape(B, S, H, DIFF_V_DIM)
    o = rms_norm(o, sub_g) * (1.0 - lambda_init)
    return o.reshape(B, S, H * DIFF_V_DIM) @ w_out


def moba_attention(h, w_in, w_out, cos, sin):
    B, S, _ = h.shape
    H, D, Bk = MOBA_HEADS, HEAD_DIM, MOBA_BLOCK
    f32 = jnp.float32
    q, k, v = jnp.split(h @ w_in, 3, axis=-1)
    q = apply_partial_rope(q.reshape(B, S, H, D), cos, sin) * (D ** -0.5)
    k = apply_partial_rope(k.reshape(B, S, H, D), cos, sin)
    v = v.reshape(B, S, H, D)
    nb = -(-S // Bk)
    pad = nb * Bk - S
    k = jnp.pad(k, ((0, 0), (0, pad), (0, 0), (0, 0)))
    v = jnp.pad(v, ((0, 0), (0, pad), (0, 0), (0, 0)))
    K = min(MOBA_TOPK, nb)
    kb = k.reshape(B, nb, Bk, H, D).transpose(0, 3, 1, 2, 4)
    vb = v.reshape(B, nb, Bk, H, D).transpose(0, 3, 1, 2, 4)
    k_mean = jnp.mean(kb.astype(f32), axis=3).astype(k.dtype)
    nqc = S // MOBA_Q_CHUNK
    qc = q.reshape(B, nqc, MOBA_Q_CHUNK, H, D).transpose(1, 0, 3, 2, 4)
    b_idx = jnp.arange(B)[:, None, None, None]
    h_idx = jnp.arange(H)[None, :, None, None]
    blk_ids = jnp.arange(nb)
    in_blk = jnp.arange(Bk)

    def chunk(args):
        qi, ci = args
        q_pos = ci * MOBA_Q_CHUNK + jnp.arange(MOBA_Q_CHUNK)
        own = (ci * MOBA_Q_CHUNK) // Bk
        gate = jnp.einsum('bhqd,bhnd->bhqn', qi, k_mean).astype(f32)
        gate = jnp.where(blk_ids < own, gate, -jnp.inf)
        _, sel = lax.top_k(gate, K)
        valid = jnp.arange(K) < own
        ks = kb[b_idx, h_idx, sel]
        vs = vb[b_idx, h_idx, sel]
        s_sel = jnp.einsum('bhqd,bhqnkd->bhqnk', qi, ks).astype(f32)
        s_sel = jnp.where(valid[:, None], s_sel, -jnp.inf).reshape(B, H, MOBA_Q_CHUNK, K * Bk)
        k_own = lax.dynamic_index_in_dim(kb, own, axis=2, keepdims=False)
        v_own = lax.dynamic_index_in_dim(vb, own, axis=2, keepdims=False)
        s_own = jnp.einsum('bhqd,bhkd->bhqk', qi, k_own).astype(f32)
        own_pos = own * Bk + in_blk
        s_own = jnp.where(own_pos[None, :] <= q_pos[:, None], s_own, -jnp.inf)
        p = jax.nn.softmax(jnp.concatenate([s_sel, s_own], axis=-1), axis=-1).astype(v.dtype)
        p_sel = p[..., :K * Bk].reshape(B, H, MOBA_Q_CHUNK, K, Bk)
        p_own = p[..., K * Bk:]
        return (jnp.einsum('bhqnk,bhqnkd->bhqd', p_sel, vs)
                + jnp.einsum('bhqk,bhkd->bhqd', p_own, v_own))

    o = lax.map(chunk, (qc, jnp.arange(nqc)))
    o = o.transpose(1, 0, 3, 2, 4).reshape(B, S, H * D)
    return o @ w_out


def swa_sink_attention(h, w_in, b_in, sinks, w_out, cos, sin):
    B, S, _ = h.shape
    KV, D, Qb = SWA_KV_HEADS, HEAD_DIM, SWA_Q_BLOCK
    G = SWA_HEADS // KV
    f32 = jnp.float32
    qkv = h @ w_in + b_in
    q, k, v = jnp.split(qkv, [SWA_HEADS * D, (SWA_HEADS + KV) * D], axis=-1)
    q = apply_partial_rope(q.reshape(B, S, SWA_HEADS, D), cos, sin) * (D ** -0.5)
    k = apply_partial_rope(k.reshape(B, S, KV, D), cos, sin)
    v = v.reshape(B, S, KV, D)
    nb = S // Qb
    qb = q.reshape(B, nb, Qb, KV, G, D)
    kb = k.reshape(B, nb, Qb, KV, D)
    vb = v.reshape(B, nb, Qb, KV, D)
    pad_k = jnp.zeros_like(kb[:, :1])
    pad_v = jnp.zeros_like(vb[:, :1])
    k2 = jnp.concatenate([jnp.concatenate([pad_k, kb[:, :-1]], axis=1), kb], axis=2)
    v2 = jnp.concatenate([jnp.concatenate([pad_v, vb[:, :-1]], axis=1), vb], axis=2)
    s = jnp.einsum('bnqkgd,bnjkd->bnkgqj', qb, k2).astype(f32)
    qi = jnp.arange(Qb)[:, None] + Qb
    kj = jnp.arange(2 * Qb)[None, :]
    dist = qi - kj
    band = (dist >= 0) & (dist < SWA_WINDOW)
    prev_ok = (jnp.arange(nb)[:, None] > 0) | (jnp.arange(2 * Qb)[None, :] >= Qb)
    mask = band[None, :, :] & prev_ok[:, None, :]
    s = jnp.where(mask[None, :, None, None], s, -jnp.inf)
    sink = sinks.astype(f32).reshape(KV, G)[None, None, :, :, None, None]
    m = jnp.maximum(jnp.max(s, axis=-1, keepdims=True), sink)
    e = jnp.exp(s - m)
    p = (e / (jnp.sum(e, axis=-1, keepdims=True) + jnp.exp(sink - m))).astype(v.dtype)
    o = jnp.einsum('bnkgqj,bnjkd->bnqkgd', p, v2).reshape(B, S, SWA_HEADS * D)
    return o @ w_out


def sqrelu_mlp(h, w_up, w_down):
    a = jnp.maximum(h @ w_up, 0)
    return (a * a) @ w_down


def setup_inputs(seed: int = 0) -> dict:
    key = jax.random.key(seed)
    ks = jax.random.split(key, 24)
    f32 = jnp.float32
    nrm = lambda k, shape, scale: jax.random.normal(k, shape, f32) * scale
    x = jax.random.normal(ks[0], (BATCH, SEQ, D_MODEL), f32)
    positions = jnp.broadcast_to(jnp.arange(SEQ, dtype=jnp.int32)[None, :], (BATCH, SEQ))
    return {
        "x": x,
        "positions": positions,
        "attn_norm_g": 1.0 + nrm(ks[1], (DEPTH, D_MODEL), 0.02),
        "mlp_norm_g": 1.0 + nrm(ks[2], (DEPTH, D_MODEL), 0.02),
        "diff_w_in": nrm(ks[3], (N_DIFF, D_MODEL, DIFF_IN), D_MODEL ** -0.5),
        "diff_w_out": nrm(ks[4], (N_DIFF, DIFF_HEADS * DIFF_V_DIM, D_MODEL), (DIFF_HEADS * DIFF_V_DIM) ** -0.5),
        "diff_lam_q1": nrm(ks[5], (N_DIFF, HEAD_DIM), 0.1),
        "diff_lam_k1": nrm(ks[6], (N_DIFF, HEAD_DIM), 0.1),
        "diff_lam_q2": nrm(ks[7], (N_DIFF, HEAD_DIM), 0.1),
        "diff_lam_k2": nrm(ks[8], (N_DIFF, HEAD_DIM), 0.1),
        "diff_subln_g": 1.0 + nrm(ks[9], (N_DIFF, DIFF_V_DIM), 0.02),
        "moba_w_in": nrm(ks[10], (N_MOBA, D_MODEL, MOBA_IN), D_MODEL ** -0.5),
        "moba_w_out": nrm(ks[11], (N_MOBA, MOBA_HEADS * HEAD_DIM, D_MODEL), (MOBA_HEADS * HEAD_DIM) ** -0.5),
        "swa_w_in": nrm(ks[12], (N_SWA, D_MODEL, SWA_IN), D_MODEL ** -0.5),
        "swa_b_in": nrm(ks[13], (N_SWA, SWA_IN), 0.02),
        "swa_sinks": nrm(ks[14], (N_SWA, SWA_HEADS), 0.5),
        "swa_w_out": nrm(ks[15], (N_SWA, SWA_HEADS * HEAD_DIM, D_MODEL), (SWA_HEADS * HEAD_DIM) ** -0.5),
        "mlp_w_up": nrm(ks[16], (DEPTH, D_MODEL, D_FF), D_MODEL ** -0.5),
        "mlp_w_down": nrm(ks[17], (DEPTH, D_FF, D_MODEL), 0.5 * D_FF ** -0.5),
        "final_norm_g": 1.0 + nrm(ks[18], (D_MODEL,), 0.02),
    }


def reference(x, positions, attn_norm_g, mlp_norm_g, diff_w_in, diff_w_out, diff_lam_q1, diff_lam_k1,
              diff_lam_q2, diff_lam_k2, diff_subln_g, moba_w_in, moba_w_out, swa_w_in, swa_b_in,
              swa_sinks, swa_w_out, mlp_w_up, mlp_w_down, final_norm_g):
    cos, sin = rope_tables(positions)
    h = x
    for i in range(DEPTH):
        mixer = i % N_MIXERS
        slot = i // N_MIXERS
        a = rms_norm(h, attn_norm_g[i])
        if mixer == 0:
            lambda_init = 0.8 - 0.6 * math.exp(-0.3 * i)
            y = diff_attention(a, diff_w_in[slot], diff_w_out[slot], diff_lam_q1[slot], diff_lam_k1[slot],
                               diff_lam_q2[slot], diff_lam_k2[slot], diff_subln_g[slot], cos, sin, lambda_init)
        elif mixer == 1:
            y = moba_attention(a, moba_w_in[slot], moba_w_out[slot], cos, sin)
        else:
            y = swa_sink_attention(a, swa_w_in[slot], swa_b_in[slot], swa_sinks[slot], swa_w_out[slot], cos, sin)
        h = h + y
        h = h + sqrelu_mlp(rms_norm(h, mlp_norm_g[i]), mlp_w_up[i], mlp_w_down[i])
    return rms_norm(h, final_norm_g)
```

```python
import math
from contextlib import ExitStack
import numpy as np
import ml_dtypes
import concourse.bass as bass
import concourse.mybir as mybir
from concourse.bass_utils import run_bass_kernel_spmd

F32 = mybir.dt.float32
BF16 = mybir.dt.bfloat16
I32 = mybir.dt.int32
AF = mybir.ActivationFunctionType
ALU = mybir.AluOpType
AX = mybir.AxisListType
NPBF = ml_dtypes.bfloat16

D = 1024
DFF = 4096
HD = 64
EPS = 1e-6
THETA = 500000.0
NCORES = 8
BIG = 30000.0


class Buf:
    def __init__(self, ap=None):
        self.ap = ap
        self.w = None
        self.r = []


def _flat(xs):
    out = []
    for x in xs:
        if isinstance(x, (list, tuple)):
            out.extend(_flat(x))
        else:
            out.append(x)
    return out


class Eng:
    def __init__(self, name, eng, sem):
        self.name, self.eng, self.sem = name, eng, sem
        self.cnt = 0
        self.seen = {}


class KB:
    def __init__(self):
        self.nc = bass.Bass("TRN2", target_bir_lowering=False)
        self.es = ExitStack()
        nc = self.nc
        self.E = {}
        for nm, e in (("pe", nc.tensor), ("act", nc.scalar), ("dve", nc.vector), ("pool", nc.gpsimd), ("sp", nc.sync)):
            self.E[nm] = Eng(nm, e, self.es.enter_context(nc.semaphore("sem_" + nm)))
        self.dsem = [[self.es.enter_context(nc.semaphore("dsem%d" % i)), 0] for i in range(40)]
        self.dnext = 0
        self.uid = 0

    def sb(self, shape, dt, name=None):
        self.uid += 1
        t = self.es.enter_context(self.nc.sbuf_tensor((name or "sb") + "_s%d" % self.uid, list(shape), dt))
        return t

    def ps(self, shape, dt, name=None):
        self.uid += 1
        return self.es.enter_context(self.nc.psum_tensor((name or "ps") + "_p%d" % self.uid, list(shape), dt))

    def _sync(self, E, reads, writes):
        deps = []
        for b in reads:
            if b.w is not None:
                deps.append(b.w)
        for b in writes:
            if b.w is not None:
                deps.append(b.w)
            deps.extend(b.r)
        need = {}
        for (sem, val, owner) in deps:
            if owner is E:
                if E.name == "pe":
                    continue
                if val <= E.cnt - 2:
                    continue
            k = id(sem)
            if k not in need or need[k][1] < val:
                need[k] = (sem, val)
        for k, (sem, val) in need.items():
            if E.seen.get(k, 0) < val:
                E.eng.wait_ge(sem, val)
                E.seen[k] = val

    def op(self, en, fn, reads=(), writes=()):
        reads = _flat(reads); writes = _flat(writes)
        E = self.E[en]
        self._sync(E, reads, writes)
        ins = fn(E.eng)
        ins.then_inc(E.sem, 1)
        E.cnt += 1
        ev = (E.sem, E.cnt, E)
        for b in reads:
            b.r.append(ev)
        for b in writes:
            b.w = ev
            b.r = []
        return ev

    def mms(self, fns, reads=(), writes=()):
        reads = _flat(reads); writes = _flat(writes)
        E = self.E["pe"]
        self._sync(E, reads, writes)
        ins = None
        for f in fns:
            ins = f(E.eng)
        ins.then_inc(E.sem, 1)
        E.cnt += 1
        ev = (E.sem, E.cnt, E)
        for b in reads:
            b.r.append(ev)
        for b in writes:
            b.w = ev
            b.r = []
        return ev

    def dma(self, en, out, in_, reads=(), writes=()):
        reads = _flat(reads); writes = _flat(writes)
        E = self.E[en]
        self._sync(E, reads, writes)
        slot = self.dsem[self.dnext]
        self.dnext = (self.dnext + 1) % len(self.dsem)
        sem, cnt = slot
        k = id(sem)
        if cnt > 0 and E.seen.get(k, 0) < cnt:
            E.eng.wait_ge(sem, cnt)
            E.seen[k] = cnt
        E.eng.dma_start(out=out, in_=in_).then_inc(sem, 16)
        slot[1] = cnt + 16
        ev = (sem, cnt + 16, None)
        for b in reads:
            b.r.append(ev)
        for b in writes:
            b.w = ev
            b.r = []
        return ev

    def finish(self, bufs):
        E = self.E["sp"]
        self._sync(E, [], bufs)
        for nm, o in self.E.items():
            if o.cnt > 0 and E.seen.get(id(o.sem), 0) < o.cnt:
                E.eng.wait_ge(o.sem, o.cnt)
        for sem, cnt in self.dsem:
            if cnt > 0 and E.seen.get(id(sem), 0) < cnt:
                E.eng.wait_ge(sem, cnt)
        self.es.close()
        return self.nc


def make_identity(kb, dt=BF16):
    idn = kb.sb([128, 128], dt, "idn")
    b = Buf()
    kb.op("pool", lambda e: e.memset(idn[:, :], 0.0), writes=[b])
    kb.op("pool", lambda e: e.affine_select(out=idn[:, :], in_=idn[:, :], pattern=[[-1, 128]], compare_op=ALU.not_equal,
                                            fill=1.0, base=0, channel_multiplier=1), reads=[b], writes=[b])
    return idn, b


def rms_rstd(kb, x_ap, xb, sq, sqb, ss, ssb, n):
    kb.op("act", lambda e: e.activation(out=sq, in_=x_ap, func=AF.Square, accum_out=ss), reads=[xb], writes=[sqb, ssb])
    kb.op("dve", lambda e: e.tensor_scalar(out=ss, in0=ss, scalar1=1.0 / n, scalar2=EPS, op0=ALU.mult, op1=ALU.add),
          reads=[ssb], writes=[ssb])
    kb.op("act", lambda e: e.activation(out=ss, in_=ss, func=AF.Sqrt), reads=[ssb], writes=[ssb])
    kb.op("dve", lambda e: e.reciprocal(out=ss, in_=ss), reads=[ssb], writes=[ssb])


def build_pre(T, NCOL, n_rot_heads, has_bias):
    kb = KB()
    nc = kb.nc
    h = nc.dram_tensor("h", [T, D], F32, kind="ExternalInput").ap()
    pos = nc.dram_tensor("pos", [T, 1], I32, kind="ExternalInput").ap()
    g = nc.dram_tensor("g", [128, D], F32, kind="ExternalInput").ap()
    inv = nc.dram_tensor("inv", [128, 8], F32, kind="ExternalInput").ap()
    w = nc.dram_tensor("w", [D, NCOL], F32, kind="ExternalInput").ap()
    if has_bias:
        bias = nc.dram_tensor("bias", [128, NCOL], F32, kind="ExternalInput").ap()
    out = nc.dram_tensor("qkv", [T, NCOL], BF16, kind="ExternalOutput").ap()
    outb = Buf()

    idn, idnb = make_identity(kb)
    g_sb = kb.sb([128, D], F32, "g_sb"); gb = Buf()
    kb.dma("sp", g_sb[:, :], g[:, :], writes=[gb])
    inv_sb = kb.sb([128, 8], F32, "inv_sb"); invb = Buf()
    kb.dma("sp", inv_sb[:, :], inv[:, :], writes=[invb])
    if has_bias:
        b_sb = kb.sb([128, NCOL], F32, "b_sb"); bb = Buf()
        kb.dma("sp", b_sb[:, :], bias[:, :], writes=[bb])
    w_sb = kb.sb([128, 8, NCOL], BF16, "w_sb"); wb = [Buf() for _ in range(8)]
    wv = w.rearrange("(kc p) n -> p kc n", p=128)
    for kc in range(8):
        kb.dma("pool", w_sb[:, kc, :], wv[:, kc, :], writes=[wb[kc]])

    NB = 2
    h_sb = [kb.sb([128, D], F32, "h_sb%d" % i) for i in range(NB)]; hb = [Buf() for _ in range(NB)]
    sq = kb.sb([128, D], F32, "sq"); sqb = Buf()
    ss = kb.sb([128, 1], F32, "ss"); ssb = Buf()
    a_sb = kb.sb([128, D], BF16, "a_sb"); ab = Buf()
    aT = kb.sb([128, 8, 128], BF16, "aT"); aTb = Buf()
    tp = kb.ps([128, 8, 128], BF16, "tp"); tpb = Buf()
    NG = (NCOL + 511) // 512
    pp = [kb.ps([128, 512], F32, "pp%d" % i) for i in range(2)]; ppb = [Buf() for _ in range(2)]
    qkv = kb.sb([128, NCOL], F32, "qkv_sb"); qb = Buf()
    qo = [kb.sb([128, NCOL], BF16, "qo%d" % i) for i in range(2)]; qob = [Buf() for _ in range(2)]
    pos_i = kb.sb([128, 1], I32, "pos_i"); pib = Buf()
    pos_f = kb.sb([128, 1], F32, "pos_f"); pfb = Buf()
    ang = kb.sb([128, 8], F32, "ang"); angb = Buf()
    cs = kb.sb([128, 2, 8], F32, "cs"); csb = Buf()
    arg = kb.sb([128, 2, 8], F32, "arg"); argb = Buf()
    ki = kb.sb([128, 2, 8], I32, "ki"); kib = Buf()
    kf = kb.sb([128, 2, 8], F32, "kf"); kfb = Buf()
    nh = n_rot_heads
    tmp = kb.sb([128, 4, nh, 8], F32, "ropetmp"); tmpb = Buf()
    TWO_PI = 2.0 * math.pi

    nt = T // 128
    for t in range(nt):
        s = t % NB
        kb.dma("sp", h_sb[s][:, :], h[t * 128:(t + 1) * 128, :], writes=[hb[s]])
        kb.dma("sp", pos_i[:, :], pos[t * 128:(t + 1) * 128, :], writes=[pib])
        kb.op("dve", lambda e: e.tensor_copy(out=pos_f[:, :], in_=pos_i[:, :]), reads=[pib], writes=[pfb])
        kb.op("dve", lambda e: e.tensor_scalar(out=ang[:, :], in0=inv_sb[:, :], scalar1=pos_f[:, 0:1], scalar2=None, op0=ALU.mult),
              reads=[pfb, invb], writes=[angb])
        kb.op("dve", lambda e: e.tensor_copy(out=arg[:, 1, :], in_=ang[:, :]), reads=[angb], writes=[argb])
        kb.op("dve", lambda e: e.tensor_scalar(out=arg[:, 0, :], in0=ang[:, :], scalar1=0.5 * math.pi, scalar2=None, op0=ALU.add),
              reads=[angb], writes=[argb])
        kb.op("dve", lambda e: e.tensor_scalar(out=ki[:, :, :], in0=arg[:, :, :], scalar1=1.0 / TWO_PI, scalar2=None, op0=ALU.mult),
              reads=[argb], writes=[kib])
        kb.op("dve", lambda e: e.tensor_copy(out=kf[:, :, :], in_=ki[:, :, :]), reads=[kib], writes=[kfb])
        kb.op("dve", lambda e: e.scalar_tensor_tensor(out=cs[:, :, :], in0=kf[:, :, :], scalar=-TWO_PI, in1=arg[:, :, :], op0=ALU.mult, op1=ALU.add),
              reads=[kfb, argb], writes=[csb])
        kb.op("dve", lambda e: e.tensor_scalar(out=kf[:, :, :], in0=cs[:, :, :], scalar1=math.pi, scalar2=-TWO_PI, op0=ALU.is_gt, op1=ALU.mult),
              reads=[csb], writes=[kfb])
        kb.op("dve", lambda e: e.tensor_tensor(out=cs[:, :, :], in0=cs[:, :, :], in1=kf[:, :, :], op=ALU.add), reads=[csb, kfb], writes=[csb])
        kb.op("dve", lambda e: e.tensor_scalar(out=kf[:, :, :], in0=cs[:, :, :], scalar1=-math.pi, scalar2=TWO_PI, op0=ALU.is_lt, op1=ALU.mult),
              reads=[csb], writes=[kfb])
        kb.op("dve", lambda e: e.tensor_tensor(out=cs[:, :, :], in0=cs[:, :, :], in1=kf[:, :, :], op=ALU.add), reads=[csb, kfb], writes=[csb])
        kb.op("act", lambda e: e.activation(out=cs[:, :, :], in_=cs[:, :, :], func=AF.Sin), reads=[csb], writes=[csb])
        rms_rstd(kb, h_sb[s][:, :], hb[s], sq[:, :], sqb, ss[:, 0:1], ssb, D)
        kb.op("dve", lambda e: e.scalar_tensor_tensor(out=a_sb[:, :], in0=h_sb[s][:, :], scalar=ss[:, 0:1], in1=g_sb[:, :],
                                                      op0=ALU.mult, op1=ALU.mult), reads=[hb[s], ssb, gb], writes=[ab])
        kb.mms([(lambda e, kc=kc: e.transpose(out=tp[:, kc, :], in_=a_sb[:, kc * 128:(kc + 1) * 128], identity=idn[:, :]))
                for kc in range(8)], reads=[ab, idnb], writes=[tpb])
        kb.op("act", lambda e: e.activation(out=aT[:, :, :], in_=tp[:, :, :], func=AF.Copy), reads=[tpb], writes=[aTb])
        for n in range(NG):
            c0 = n * 512; c1 = min(NCOL, c0 + 512); cw = c1 - c0
            p = pp[n % 2]; pb = ppb[n % 2]
            kb.mms([(lambda e, kc=kc: e.matmul(p[:, 0:cw], lhsT=aT[:, kc, :], rhs=w_sb[:, kc, c0:c1], start=(kc == 0), stop=(kc == 7)))
                    for kc in range(8)], reads=[aTb, wb], writes=[pb])
            if has_bias:
                kb.op("dve", lambda e: e.tensor_tensor(out=qkv[:, c0:c1], in0=p[:, 0:cw], in1=b_sb[:, c0:c1], op=ALU.add),
                      reads=[pb, bb], writes=[qb])
            else:
                kb.op("dve" if n % 2 == 0 else "act",
                      (lambda e: e.tensor_copy(out=qkv[:, c0:c1], in_=p[:, 0:cw])) if n % 2 == 0 else
                      (lambda e: e.activation(out=qkv[:, c0:c1], in_=p[:, 0:cw], func=AF.Copy)),
                      reads=[pb], writes=[qb])
        xv = qkv[:, 0:nh * 64].rearrange("p (h d) -> p h d", d=64)
        x1 = xv[:, :, 0:8]; x2 = xv[:, :, 8:16]
        cb = cs[:, 0:1, :].to_broadcast([128, nh, 8]); sbb = cs[:, 1:2, :].to_broadcast([128, nh, 8])
        kb.op("dve", lambda e: e.tensor_tensor(out=tmp[:, 0, :, :], in0=x1, in1=cb, op=ALU.mult), reads=[qb, csb], writes=[tmpb])
        kb.op("dve", lambda e: e.tensor_tensor(out=tmp[:, 1, :, :], in0=x2, in1=sbb, op=ALU.mult), reads=[qb, csb], writes=[tmpb])
        kb.op("pool", lambda e: e.tensor_tensor(out=tmp[:, 2, :, :], in0=x2, in1=cb, op=ALU.mult), reads=[qb, csb], writes=[tmpb])
        kb.op("pool", lambda e: e.tensor_tensor(out=tmp[:, 3, :, :], in0=x1, in1=sbb, op=ALU.mult), reads=[qb, csb], writes=[tmpb])
        kb.op("dve", lambda e: e.tensor_tensor(out=x1, in0=tmp[:, 0, :, :], in1=tmp[:, 1, :, :], op=ALU.subtract), reads=[tmpb], writes=[qb])
        kb.op("dve", lambda e: e.tensor_tensor(out=x2, in0=tmp[:, 2, :, :], in1=tmp[:, 3, :, :], op=ALU.add), reads=[tmpb], writes=[qb])
        o = qo[t % 2]; ob = qob[t % 2]
        kb.op("act", lambda e: e.activation(out=o[:, :], in_=qkv[:, :], func=AF.Copy), reads=[qb], writes=[ob])
        kb.dma("sp", out[t * 128:(t + 1) * 128, :], o[:, :], reads=[ob])
    return kb.finish([outb])


def build_post(T, final):
    kb = KB()
    nc = kb.nc
    TG = 256
    NS = TG // 128
    o = nc.dram_tensor("o", [T, D], BF16, kind="ExternalInput").ap()
    h = nc.dram_tensor("h", [T, D], F32, kind="ExternalInput").ap()
    g = nc.dram_tensor("g", [128, D], F32, kind="ExternalInput").ap()
    w_out = nc.dram_tensor("w_out", [D, D], F32, kind="ExternalInput").ap()
    w_up = nc.dram_tensor("w_up", [D, DFF], F32, kind="ExternalInput").ap()
    w_dn = nc.dram_tensor("w_dn", [DFF, D], F32, kind="ExternalInput").ap()
    if final:
        gf = nc.dram_tensor("gf", [128, D], F32, kind="ExternalInput").ap()
    out = nc.dram_tensor("hout", [T, D], F32, kind="ExternalOutput").ap()
    outb = Buf()

    idn, idnb = make_identity(kb)
    g_sb = kb.sb([128, D], F32, "g_sb"); gb = Buf()
    kb.dma("sp", g_sb[:, :], g[:, :], writes=[gb])
    if final:
        gf_sb = kb.sb([128, D], F32, "gf_sb"); gfb = Buf()
        kb.dma("sp", gf_sb[:, :], gf[:, :], writes=[gfb])
    wo_sb = kb.sb([128, 8, D], BF16, "wo_sb"); wob = [Buf() for _ in range(8)]
    wu_sb = kb.sb([128, 8, DFF], BF16, "wu_sb"); wub = [Buf() for _ in range(8)]
    wd_sb = kb.sb([128, 32, D], BF16, "wd_sb"); wdb = [Buf() for _ in range(32)]
    wov = w_out.rearrange("(kc p) n -> p kc n", p=128)
    wuv = w_up.rearrange("(kc p) n -> p kc n", p=128)
    wdv = w_dn.rearrange("(kc p) n -> p kc n", p=128)
    for kc in range(8):
        kb.dma("pool", wo_sb[:, kc, :], wov[:, kc, :], writes=[wob[kc]])
    for kc in range(8):
        kb.dma("pool", wu_sb[:, kc, :], wuv[:, kc, :], writes=[wub[kc]])
    for kc in range(32):
        kb.dma("pool", wd_sb[:, kc, :], wdv[:, kc, :], writes=[wdb[kc]])

    h1 = kb.sb([128, NS, D], F32, "h1"); h1b = [Buf() for _ in range(NS)]
    ob16 = kb.sb([128, NS, D], BF16, "ob16"); obb = [Buf() for _ in range(NS)]
    xT = kb.sb([128, 8, TG], BF16, "xT"); xTb = Buf()
    hidT = kb.sb([128, 32, TG], BF16, "hidT"); hidb = Buf()
    sq = kb.sb([128, D], F32, "sq"); sqb = Buf()
    ss = kb.sb([128, 1], F32, "ss"); ssb = Buf()
    rl = [kb.sb([128, TG], F32, "rl%d" % i) for i in range(2)]; rlb = [Buf() for _ in range(2)]
    tp = kb.ps([128, 8, 128], BF16, "tp"); tpb = Buf()
    pp = [kb.ps([128, 512], F32, "pp%d" % i) for i in range(4)]; ppb = [Buf() for _ in range(4)]
    pi = [0]

    def nextp():
        i = pi[0] % 4
        pi[0] += 1
        return pp[i], ppb[i]

    def transposes(src_tile, srcb, si):
        kb.mms([(lambda e, kc=kc: e.transpose(out=tp[:, kc, :], in_=src_tile[:, kc * 128:(kc + 1) * 128], identity=idn[:, :]))
                for kc in range(8)], reads=[srcb, idnb], writes=[tpb])
        kb.op("act", lambda e: e.activation(out=xT[:, :, si * 128:(si + 1) * 128], in_=tp[:, :, :], func=AF.Copy),
              reads=[tpb], writes=[xTb])

    for tg in range(T // TG):
        r0 = tg * TG
        for si in range(NS):
            kb.dma("sp", ob16[:, si, :], o[r0 + si * 128:r0 + (si + 1) * 128, :], writes=[obb[si]])
            kb.dma("sp", h1[:, si, :], h[r0 + si * 128:r0 + (si + 1) * 128, :], writes=[h1b[si]])
        for si in range(NS):
            transposes(ob16[:, si, :], obb[si], si)
        for si in range(NS):
            for n in range(2):
                p, pb = nextp()
                kb.mms([(lambda e, kc=kc: e.matmul(p[:, :], lhsT=xT[:, kc, si * 128:(si + 1) * 128], rhs=wo_sb[:, kc, n * 512:(n + 1) * 512],
                                                   start=(kc == 0), stop=(kc == 7))) for kc in range(8)], reads=[xTb, wob], writes=[pb])
                kb.op("dve", lambda e: e.tensor_tensor(out=h1[:, si, n * 512:(n + 1) * 512], in0=p[:, :], in1=h1[:, si, n * 512:(n + 1) * 512],
                                                       op=ALU.add), reads=[pb, h1b[si]], writes=[h1b[si]])
        for si in range(NS):
            rms_rstd(kb, h1[:, si, :], h1b[si], sq[:, :], sqb, ss[:, 0:1], ssb, D)
            kb.op("dve", lambda e: e.scalar_tensor_tensor(out=ob16[:, si, :], in0=h1[:, si, :], scalar=ss[:, 0:1], in1=g_sb[:, :],
                                                          op0=ALU.mult, op1=ALU.mult), reads=[h1b[si], ssb, gb], writes=[obb[si]])
        for si in range(NS):
            transposes(ob16[:, si, :], obb[si], si)
        for fc in range(32):
            p, pb = nextp()
            kb.mms([(lambda e, kc=kc: e.matmul(p[:, 0:TG], lhsT=wu_sb[:, kc, fc * 128:(fc + 1) * 128], rhs=xT[:, kc, :],
                                               start=(kc == 0), stop=(kc == 7))) for kc in range(8)], reads=[xTb, wub], writes=[pb])
            r = rl[fc % 2]; rb = rlb[fc % 2]
            kb.op("act", lambda e: e.activation(out=r[:, :], in_=p[:, 0:TG], func=AF.Relu), reads=[pb], writes=[rb])
            kb.op("dve" if fc % 2 == 0 else "pool", lambda e: e.tensor_tensor(out=hidT[:, fc, :], in0=r[:, :], in1=r[:, :], op=ALU.mult),
                  reads=[rb], writes=[hidb])
        for si in range(NS):
            for n in range(2):
                p, pb = nextp()
                kb.mms([(lambda e, fc=fc: e.matmul(p[:, :], lhsT=hidT[:, fc, si * 128:(si + 1) * 128], rhs=wd_sb[:, fc, n * 512:(n + 1) * 512],
                                                   start=(fc == 0), stop=(fc == 31))) for fc in range(32)], reads=[hidb, wdb], writes=[pb])
                kb.op("dve", lambda e: e.tensor_tensor(out=h1[:, si, n * 512:(n + 1) * 512], in0=p[:, :], in1=h1[:, si, n * 512:(n + 1) * 512],
                                                       op=ALU.add), reads=[pb, h1b[si]], writes=[h1b[si]])
            if final:
                rms_rstd(kb, h1[:, si, :], h1b[si], sq[:, :], sqb, ss[:, 0:1], ssb, D)
                kb.op("dve", lambda e: e.scalar_tensor_tensor(out=h1[:, si, :], in0=h1[:, si, :], scalar=ss[:, 0:1], in1=gf_sb[:, :],
                                                              op0=ALU.mult, op1=ALU.mult), reads=[h1b[si], ssb, gfb], writes=[h1b[si]])
            kb.dma("sp", out[r0 + si * 128:r0 + (si + 1) * 128, :], h1[:, si, :], reads=[h1b[si]])
    return kb.finish([outb])


def make_masks(kb):
    cm = kb.sb([128, 128], BF16, "cmask"); cmb = Buf()
    am = kb.sb([128, 128], BF16, "amask"); amb = Buf()
    kb.op("pool", lambda e: e.memset(cm[:, :], 1.0), writes=[cmb])
    kb.op("pool", lambda e: e.affine_select(out=cm[:, :], in_=cm[:, :], pattern=[[1, 128]], compare_op=ALU.is_ge,
                                            fill=0.0, base=0, channel_multiplier=-1), reads=[cmb], writes=[cmb])
    kb.op("pool", lambda e: e.memset(am[:, :], 1.0), writes=[amb])
    kb.op("pool", lambda e: e.affine_select(out=am[:, :], in_=am[:, :], pattern=[[-1, 128]], compare_op=ALU.is_gt,
                                            fill=0.0, base=0, channel_multiplier=1), reads=[amb], writes=[amb])
    return cm, cmb, am, amb


def causal_attn_tile(kb, S, t, qT, qTb, kT, kTb, KC, V, Vb, DV, st, stb, pt, ptb, acc, accb, cm, cmb, cnt):
    nq = min(4, S // 128 - 4 * t)
    last = 4 * t + nq - 1
    for j in range(last + 1):
        jj = j - 4 * t
        i0 = max(0, jj)
        qs = i0 * 128
        b = cnt[0] % len(st); cnt[0] += 1
        kb.mms([lambda e: e.matmul(st[b][:, qs:nq * 128], lhsT=kT[0:KC, j * 128:(j + 1) * 128], rhs=qT[0:KC, t * 512 + qs:t * 512 + nq * 128],
                                   start=True, stop=True)], reads=[qTb, kTb], writes=[stb[b]])
        kb.op("act", lambda e: e.activation(out=pt[b][:, qs:nq * 128], in_=st[b][:, qs:nq * 128], func=AF.Exp, scale=0.125),
              reads=[stb[b]], writes=[ptb[b]])
        if jj >= 0:
            kb.op("pool", lambda e: e.tensor_tensor(out=pt[b][:, qs:qs + 128], in0=pt[b][:, qs:qs + 128], in1=cm[:, :], op=ALU.mult),
                  reads=[ptb[b], cmb], writes=[ptb[b]])
        for i in range(i0, nq):
            kb.mms([lambda e, i=i: e.matmul(acc[i][:, 0:DV + 1], lhsT=pt[b][:, i * 128:(i + 1) * 128], rhs=V[:, j, :],
                                            start=(j == 0), stop=(j == 4 * t + i))], reads=[ptb[b], Vb], writes=[accb[i]])
    return nq


def build_diff(S, NHL, lambda_init):
    kb = KB()
    nc = kb.nc
    qT_d = nc.dram_tensor("qT", [NHL * 2, 64, S], BF16, kind="ExternalInput").ap()
    kT_d = nc.dram_tensor("kT", [NHL * 2, 64, S], BF16, kind="ExternalInput").ap()
    v_d = nc.dram_tensor("v", [S, NHL * 128], BF16, kind="ExternalInput").ap()
    lam_d = nc.dram_tensor("lam", [128, 4, 64], F32, kind="ExternalInput").ap()
    sg_d = nc.dram_tensor("subg", [128, 128], F32, kind="ExternalInput").ap()
    out = nc.dram_tensor("o", [S, NHL * 128], BF16, kind="ExternalOutput").ap()
    outb = Buf()
    NT = S // 128
    cm, cmb, am, amb = make_masks(kb)
    lam_sb = kb.sb([128, 4, 64], F32, "lam_sb"); lb = Buf()
    kb.dma("sp", lam_sb[:, :, :], lam_d[:, :, :], writes=[lb])
    sg = kb.sb([128, 128], F32, "sg"); sgb = Buf()
    kb.dma("sp", sg[:, :], sg_d[:, :], writes=[sgb])
    lp = kb.sb([128, 2, 64], F32, "lp"); lpb = Buf()
    ls = kb.sb([128, 2], F32, "ls"); lsb = Buf()
    nlam = kb.sb([128, 1], F32, "nlam"); nlb = Buf()
    kb.op("dve", lambda e: e.tensor_tensor(out=lp[:, 0, :], in0=lam_sb[:, 0, :], in1=lam_sb[:, 1, :], op=ALU.mult), reads=[lb], writes=[lpb])
    kb.op("dve", lambda e: e.tensor_tensor(out=lp[:, 1, :], in0=lam_sb[:, 2, :], in1=lam_sb[:, 3, :], op=ALU.mult), reads=[lb], writes=[lpb])
    kb.op("dve", lambda e: e.tensor_reduce(out=ls[:, :], in_=lp[:, :, :], axis=AX.X, op=ALU.add), reads=[lpb], writes=[lsb])
    kb.op("act", lambda e: e.activation(out=ls[:, :], in_=ls[:, :], func=AF.Exp), reads=[lsb], writes=[lsb])
    kb.op("dve", lambda e: e.tensor_tensor(out=nlam[:, :], in0=ls[:, 1:2], in1=ls[:, 0:1], op=ALU.subtract), reads=[lsb], writes=[nlb])
    kb.op("dve", lambda e: e.tensor_scalar(out=nlam[:, :], in0=nlam[:, :], scalar1=-lambda_init, scalar2=None, op0=ALU.add),
          reads=[nlb], writes=[nlb])
    kb.op("dve", lambda e: e.tensor_scalar(out=sg[:, :], in0=sg[:, :], scalar1=1.0 - lambda_init, scalar2=None, op0=ALU.mult),
          reads=[sgb], writes=[sgb])

    qT = [kb.sb([64, S], BF16, "qT%d" % c) for c in range(2)]; qTb = [Buf() for _ in range(2)]
    kT = [kb.sb([64, S], BF16, "kT%d" % c) for c in range(2)]; kTb = [Buf() for _ in range(2)]
    V = kb.sb([128, NT, 129], BF16, "V"); Vb = [Buf() for _ in range((NT + 7) // 8)]
    st = [kb.ps([128, 512], F32, "st%d" % i) for i in range(2)]; stb = [Buf() for _ in range(2)]
    pt = [kb.sb([128, 512], BF16, "pt%d" % i) for i in range(2)]; ptb = [Buf() for _ in range(2)]
    acc = [kb.ps([128, 512], F32, "acc%d" % i) for i in range(4)]; accb = [Buf() for _ in range(4)]
    rz = kb.sb([128, 1], F32, "rz"); rzb = Buf()
    O0 = kb.sb([128, 4, 128], F32, "O0"); O0b = [Buf() for _ in range(4)]
    O1 = kb.sb([128, 128], F32, "O1"); O1b = Buf()
    sq = kb.sb([128, 128], F32, "sq"); sqb = Buf()
    ss = kb.sb([128, 1], F32, "ss"); ssb = Buf()
    ob = [kb.sb([128, 128], BF16, "ob%d" % i) for i in range(2)]; obb = [Buf() for _ in range(2)]
    cnt = [0]
    oc = 0
    vv = v_d.rearrange("(t p) c -> p t c", p=128)
    for hd in range(NHL):
        kb.op("pool", lambda e: e.memset(V[:, :, 128:129], 1.0), writes=[Vb])
        for t0 in range(0, NT, 8):
            kb.dma("sp", V[:, t0:t0 + 8, 0:128], vv[:, t0:t0 + 8, hd * 128:(hd + 1) * 128], writes=[Vb[t0 // 8]])
        for c in range(2):
            kb.dma("sp", qT[c][:, :], qT_d[hd * 2 + c, :, :], writes=[qTb[c]])
            kb.dma("sp", kT[c][:, :], kT_d[hd * 2 + c, :, :], writes=[kTb[c]])
        for t in range((S + 511) // 512):
            for c in range(2):
                nq = causal_attn_tile(kb, S, t, qT[c], qTb[c], kT[c], kTb[c], 64, V, Vb, 128, st, stb, pt, ptb, acc, accb, cm, cmb, cnt)
                for i in range(nq):
                    kb.op("dve", lambda e: e.reciprocal(out=rz[:, :], in_=acc[i][:, 128:129]), reads=[accb[i]], writes=[rzb])
                    if c == 0:
                        kb.op("dve", lambda e: e.tensor_scalar(out=O0[:, i, :], in0=acc[i][:, 0:128], scalar1=rz[:, 0:1], scalar2=None, op0=ALU.mult),
                              reads=[accb[i], rzb], writes=[O0b[i]])
                    else:
                        kb.op("dve", lambda e: e.tensor_scalar(out=O1[:, :], in0=acc[i][:, 0:128], scalar1=rz[:, 0:1], scalar2=None, op0=ALU.mult),
                              reads=[accb[i], rzb], writes=[O1b])
                        kb.op("dve", lambda e: e.scalar_tensor_tensor(out=O1[:, :], in0=O1[:, :], scalar=nlam[:, 0:1], in1=O0[:, i, :],
                                                                      op0=ALU.mult, op1=ALU.add), reads=[O1b, nlb, O0b[i]], writes=[O1b])
                        rms_rstd(kb, O1[:, :], O1b, sq[:, :], sqb, ss[:, 0:1], ssb, 128)
                        o_ = ob[oc % 2]; o_b = obb[oc % 2]; oc += 1
                        kb.op("dve", lambda e: e.scalar_tensor_tensor(out=o_[:, :], in0=O1[:, :], scalar=ss[:, 0:1], in1=sg[:, :],
                                                                      op0=ALU.mult, op1=ALU.mult), reads=[O1b, ssb, sgb], writes=[o_b])
                        r0 = t * 512 + i * 128
                        kb.dma("sp", out[r0:r0 + 128, hd * 128:(hd + 1) * 128], o_[:, :], reads=[o_b])
    return kb.finish([outb])


def build_moba(S, NHL):
    kb = KB()
    nc = kb.nc
    NB = S // 256
    NT = S // 128
    KC = 64 + 32
    qtok_d = nc.dram_tensor("qtok", [S, NHL * 64], BF16, kind="ExternalInput").ap()
    qT_d = nc.dram_tensor("qT", [NHL, 64, S], BF16, kind="ExternalInput").ap()
    kT_d = nc.dram_tensor("kT", [NHL, 64, S], BF16, kind="ExternalInput").ap()
    oh_d = nc.dram_tensor("onehot", [32, S], BF16, kind="ExternalInput").ap()
    v_d = nc.dram_tensor("v", [S, NHL * 64], BF16, kind="ExternalInput").ap()
    out = nc.dram_tensor("o", [S, NHL * 64], BF16, kind="ExternalOutput").ap()
    outb = Buf()
    cm, cmb, am, amb = make_masks(kb)
    idn, idnb = make_identity(kb)
    qa = kb.sb([KC, S], BF16, "qa"); qab = Buf()
    ka = kb.sb([KC, S], BF16, "ka"); kab = Buf()
    qTs = kb.sb([64, S], BF16, "qTs"); qTsb = Buf()
    V = kb.sb([128, NT, 65], BF16, "V"); Vb = [Buf() for _ in range((NT + 7) // 8)]
    km = kb.sb([64, 32], F32, "km"); kmb = Buf()
    kmh = kb.sb([64, 32], BF16, "kmh"); kmhb = Buf()
    qtk = kb.sb([128, NT, KC], BF16, "qtk"); qtkb = Buf()
    gp = kb.ps([128, 32], F32, "gp"); gpb = Buf()
    gs = kb.sb([128, 32], F32, "gs"); gsb = Buf()
    top8 = kb.sb([128, 8], F32, "top8"); t8b = Buf()
    thr = kb.sb([128, 1], F32, "thr"); thb = Buf()
    tpp = kb.ps([128, 128], BF16, "tpp"); tppb = Buf()
    st = [kb.ps([128, 512], F32, "st%d" % i) for i in range(2)]; stb = [Buf() for _ in range(2)]
    pt = [kb.sb([128, 512], BF16, "pt%d" % i) for i in range(2)]; ptb = [Buf() for _ in range(2)]
    acc = [kb.ps([128, 512], F32, "acc%d" % i) for i in range(4)]; accb = [Buf() for _ in range(4)]
    rz = kb.sb([128, 1], F32, "rz"); rzb = Buf()
    ob = [kb.sb([128, 64], BF16, "ob%d" % i) for i in range(2)]; obb = [Buf() for _ in range(2)]
    cnt = [0]
    oc = 0
    vv = v_d.rearrange("(t p) c -> p t c", p=128)
    qtv = qtok_d.rearrange("(t p) c -> p t c", p=128)
    kb.dma("sp", ka[64:96, :], oh_d[:, :], writes=[kab])
    for hd in range(NHL):
        kb.op("pool", lambda e: e.memset(V[:, :, 64:65], 1.0), writes=[Vb])
        for t0 in range(0, NT, 8):
            kb.dma("sp", V[:, t0:t0 + 8, 0:64], vv[:, t0:t0 + 8, hd * 64:(hd + 1) * 64], writes=[Vb[t0 // 8]])
        kb.dma("sp", ka[0:64, :], kT_d[hd, :, :], writes=[kab])
        kb.dma("sp", qTs[:, :], qT_d[hd, :, :], writes=[qTsb])
        for t0 in range(0, NT, 8):
            kb.dma("sp", qtk[:, t0:t0 + 8, 0:64], qtv[:, t0:t0 + 8, hd * 64:(hd + 1) * 64], writes=[qtkb])
        kb.op("dve", lambda e: e.tensor_reduce(out=km[:, 0:NB], in_=ka[0:64, :].rearrange("p (n k) -> p n k", k=256), axis=AX.X, op=ALU.add),
              reads=[kab], writes=[kmb])
        kb.op("dve", lambda e: e.tensor_scalar(out=kmh[:, 0:NB], in0=km[:, 0:NB], scalar1=1.0 / 256, scalar2=None, op0=ALU.mult),
              reads=[kmb], writes=[kmhb])
        for qt in range(NT):
            own = qt // 2
            if own > 0:
                kb.mms([lambda e: e.matmul(gp[:, 0:own], lhsT=qTs[:, qt * 128:(qt + 1) * 128], rhs=kmh[:, 0:own], start=True, stop=True)],
                       reads=[qTsb, kmhb], writes=[gpb])
                kb.op("dve", lambda e: e.memset(gs[:, :], -1e30), writes=[gsb])
                kb.op("dve", lambda e: e.tensor_copy(out=gs[:, 0:own], in_=gp[:, 0:own]), reads=[gpb], writes=[gsb])
                kb.op("dve", lambda e: e.max(out=top8[:, :], in_=gs[:, :]), reads=[gsb], writes=[t8b])
                kb.op("dve", lambda e: e.tensor_scalar(out=thr[:, :], in0=top8[:, 2:3], scalar1=-1e29, scalar2=None, op0=ALU.max),
                      reads=[t8b], writes=[thb])
                kb.op("dve", lambda e: e.tensor_scalar(out=gs[:, :], in0=gs[:, :], scalar1=thr[:, 0:1], scalar2=-1.0, op0=ALU.is_ge, op1=ALU.add),
                      reads=[gsb, thb], writes=[gsb])
                kb.op("dve", lambda e: e.tensor_scalar(out=qtk[:, qt, 64:96], in0=gs[:, :], scalar1=BIG, scalar2=None, op0=ALU.mult),
                      reads=[gsb], writes=[qtkb])
            else:
                kb.op("dve", lambda e: e.memset(qtk[:, qt, 64:96], -BIG), writes=[qtkb])
            kb.op("dve", lambda e: e.memset(qtk[:, qt, 64 + own:65 + own], 0.0), writes=[qtkb])
            kb.mms([lambda e: e.transpose(out=tpp[0:KC, :], in_=qtk[:, qt, :], identity=idn[:, :])], reads=[qtkb, idnb], writes=[tppb])
            kb.op("act", lambda e: e.activation(out=qa[:, qt * 128:(qt + 1) * 128], in_=tpp[0:KC, :], func=AF.Copy), reads=[tppb], writes=[qab])
        for t in range((S + 511) // 512):
            nq = causal_attn_tile(kb, S, t, qa, qab, ka, kab, KC, V, Vb, 64, st, stb, pt, ptb, acc, accb, cm, cmb, cnt)
            for i in range(nq):
                kb.op("dve", lambda e: e.reciprocal(out=rz[:, :], in_=acc[i][:, 64:65]), reads=[accb[i]], writes=[rzb])
                o_ = ob[oc % 2]; o_b = obb[oc % 2]; oc += 1
                kb.op("dve", lambda e: e.tensor_scalar(out=o_[:, :], in0=acc[i][:, 0:64], scalar1=rz[:, 0:1], scalar2=None, op0=ALU.mult),
                      reads=[accb[i], rzb], writes=[o_b])
                r0 = t * 512 + i * 128
                kb.dma("sp", out[r0:r0 + 128, hd * 64:(hd + 1) * 64], o_[:, :], reads=[o_b])
    return kb.finish([outb])


def build_swa(S, NQ):
    kb = KB()
    nc = kb.nc
    NT = S // 128
    qT_d = nc.dram_tensor("qT", [NQ, 64, S], BF16, kind="ExternalInput").ap()
    kT_d = nc.dram_tensor("kT", [64, S], BF16, kind="ExternalInput").ap()
    v_d = nc.dram_tensor("v", [S, 64], BF16, kind="ExternalInput").ap()
    sk_d = nc.dram_tensor("sinks", [128, NQ], F32, kind="ExternalInput").ap()
    out = nc.dram_tensor("o", [S, NQ * 64], BF16, kind="ExternalOutput").ap()
    outb = Buf()
    cm, cmb, am, amb = make_masks(kb)
    esk = kb.sb([128, NQ], F32, "esk"); eskb = Buf()
    kb.dma("sp", esk[:, :], sk_d[:, :], writes=[eskb])
    kb.op("act", lambda e: e.activation(out=esk[:, :], in_=esk[:, :], func=AF.Exp), reads=[eskb], writes=[eskb])
    kT = kb.sb([64, S], BF16, "kT"); kTb = Buf()
    kb.dma("sp", kT[:, :], kT_d[:, :], writes=[kTb])
    V = kb.sb([128, NT, 65], BF16, "V"); Vb = [Buf() for _ in range((NT + 7) // 8)]
    kb.op("pool", lambda e: e.memset(V[:, :, 64:65], 1.0), writes=[Vb])
    vv = v_d.rearrange("(t p) c -> p t c", p=128)
    for t0 in range(0, NT, 8):
        kb.dma("sp", V[:, t0:t0 + 8, 0:64], vv[:, t0:t0 + 8, :], writes=[Vb[t0 // 8]])
    qT = [kb.sb([64, S], BF16, "qT%d" % i) for i in range(2)]; qTb = [Buf() for _ in range(2)]
    st = [kb.ps([128, 512], F32, "st%d" % i) for i in range(2)]; stb = [Buf() for _ in range(2)]
    pt = [kb.sb([128, 256], BF16, "pt%d" % i) for i in range(2)]; ptb = [Buf() for _ in range(2)]
    acc = [kb.ps([128, 512], F32, "acc%d" % i) for i in range(2)]; accb = [Buf() for _ in range(2)]
    rz = kb.sb([128, 1], F32, "rz"); rzb = Buf()
    ob = [kb.sb([128, 64], BF16, "ob%d" % i) for i in range(2)]; obb = [Buf() for _ in range(2)]
    u = 0
    for g in range(NQ):
        q = qT[g % 2]; qb = qTb[g % 2]
        kb.dma("sp", q[:, :], qT_d[g, :, :], writes=[qb])
        for n in range(NT):
            b = u % 2; u += 1
            js = [n - 1, n] if n > 0 else [n]
            kb.mms([(lambda e, ji=ji, j=j: e.matmul(st[b][:, ji * 128:(ji + 1) * 128], lhsT=kT[:, j * 128:(j + 1) * 128], rhs=q[:, n * 128:(n + 1) * 128],
                                                    start=True, stop=True)) for ji, j in enumerate(js)], reads=[qb, kTb], writes=[stb[b]])
            w = len(js) * 128
            kb.op("act", lambda e: e.activation(out=pt[b][:, 0:w], in_=st[b][:, 0:w], func=AF.Exp, scale=0.125), reads=[stb[b]], writes=[ptb[b]])
            for ji, j in enumerate(js):
                m = cm if j == n else am
                mb = cmb if j == n else amb
                kb.op("pool" if ji == 0 else "dve", lambda e, ji=ji, m=m: e.tensor_tensor(out=pt[b][:, ji * 128:(ji + 1) * 128], in0=pt[b][:, ji * 128:(ji + 1) * 128],
                                                                                   in1=m[:, :], op=ALU.mult), reads=[ptb[b], mb], writes=[ptb[b]])
            kb.mms([(lambda e, ji=ji, j=j: e.matmul(acc[b][:, 0:65], lhsT=pt[b][:, ji * 128:(ji + 1) * 128], rhs=V[:, j, :],
                                                    start=(ji == 0), stop=(ji == len(js) - 1))) for ji, j in enumerate(js)],
                   reads=[ptb[b], Vb], writes=[accb[b]])
            kb.op("dve", lambda e: e.tensor_tensor(out=rz[:, :], in0=acc[b][:, 64:65], in1=esk[:, g:g + 1], op=ALU.add), reads=[accb[b], eskb], writes=[rzb])
            kb.op("dve", lambda e: e.reciprocal(out=rz[:, :], in_=rz[:, :]), reads=[rzb], writes=[rzb])
            o_ = ob[b]; o_b = obb[b]
            kb.op("dve", lambda e: e.tensor_scalar(out=o_[:, :], in0=acc[b][:, 0:64], scalar1=rz[:, 0:1], scalar2=None, op0=ALU.mult),
                  reads=[accb[b], rzb], writes=[o_b])
            kb.dma("sp", out[n * 128:(n + 1) * 128, g * 64:(g + 1) * 64], o_[:, :], reads=[o_b])
    return kb.finish([outb])


_CACHE = {}
DEBUG = None


def _prog(key, fn):
    if key not in _CACHE:
        _CACHE[key] = fn()
    return _CACHE[key]


def _run(nc, in_maps):
    res = run_bass_kernel_spmd(nc, in_maps, core_ids=list(range(NCORES)))
    return res.results


def _rep(v, n=128):
    v = np.asarray(v, np.float32).reshape(1, -1)
    return np.ascontiguousarray(np.broadcast_to(v, (n, v.shape[1])))


def _inv_table():
    inv = np.float32(THETA) ** (-(np.arange(0, 16, 2, dtype=np.float32) / np.float32(16)))
    return _rep(inv.astype(np.float32))


def run_layer(hs, pos, B, S, layer, mixer, P, final):
    NTOK = B * S
    T = NTOK // NCORES
    inv = _inv_table()
    posf = np.ascontiguousarray(pos.reshape(NTOK, 1).astype(np.int32))
    if mixer == 0:
        w_in = P["diff_w_in"]; ncol = 3072; nrot = 32; bias = None
    elif mixer == 1:
        w_in = P["moba_w_in"]; ncol = 3072; nrot = 32; bias = None
    else:
        w_in = P["swa_w_in"]; ncol = 1280; nrot = 18; bias = P["swa_b_in"]
    nc = _prog(("pre", T, ncol, nrot, bias is not None), lambda: build_pre(T, ncol, nrot, bias is not None))
    ims = []
    for c in range(NCORES):
        m = {"h": hs[c * T:(c + 1) * T], "pos": posf[c * T:(c + 1) * T], "g": _rep(P["attn_g"]), "inv": inv,
             "w": np.ascontiguousarray(w_in, dtype=np.float32)}
        if bias is not None:
            m["bias"] = _rep(bias)
        ims.append(m)
    res = _run(nc, ims)
    qkv = np.concatenate([np.asarray(r["qkv"]) for r in res], 0).reshape(B, S, ncol)

    assert B * 2 == NCORES
    ims = []
    if mixer == 0:
        q = qkv[:, :, 0:1024].reshape(B, S, 16, 64); k = qkv[:, :, 1024:2048].reshape(B, S, 16, 64)
        v = qkv[:, :, 2048:3072]
        lam = np.stack([_rep(P["lam_q1"]), _rep(P["lam_k1"]), _rep(P["lam_q2"]), _rep(P["lam_k2"])], 1)
        for c in range(NCORES):
            b, p = c // 2, c % 2
            ims.append({"qT": np.ascontiguousarray(q[b, :, p * 8:(p + 1) * 8, :].transpose(1, 2, 0)),
                        "kT": np.ascontiguousarray(k[b, :, p * 8:(p + 1) * 8, :].transpose(1, 2, 0)),
                        "v": np.ascontiguousarray(v[b, :, p * 512:(p + 1) * 512]),
                        "lam": np.ascontiguousarray(lam), "subg": _rep(P["subln_g"])})
        nc = _prog(("diff", S, P["lambda_init"]), lambda: build_diff(S, 4, P["lambda_init"]))
    elif mixer == 1:
        q = qkv[:, :, 0:1024].reshape(B, S, 16, 64); k = qkv[:, :, 1024:2048].reshape(B, S, 16, 64)
        v = qkv[:, :, 2048:3072]
        oh = np.zeros((32, S), NPBF)
        for n in range(S // 256):
            oh[n, n * 256:(n + 1) * 256] = 1
        for c in range(NCORES):
            b, p = c // 2, c % 2
            ims.append({"qtok": np.ascontiguousarray(qkv[b, :, p * 512:(p + 1) * 512]),
                        "qT": np.ascontiguousarray(q[b, :, p * 8:(p + 1) * 8, :].transpose(1, 2, 0)),
                        "kT": np.ascontiguousarray(k[b, :, p * 8:(p + 1) * 8, :].transpose(1, 2, 0)),
                        "onehot": oh,
                        "v": np.ascontiguousarray(v[b, :, p * 512:(p + 1) * 512])})
        nc = _prog(("moba", S), lambda: build_moba(S, 8))
    else:
        q = qkv[:, :, 0:1024].reshape(B, S, 16, 64)
        k = qkv[:, :, 1024:1152].reshape(B, S, 2, 64); v = qkv[:, :, 1152:1280].reshape(B, S, 2, 64)
        for c in range(NCORES):
            b, p = c // 2, c % 2
            ims.append({"qT": np.ascontiguousarray(q[b, :, p * 8:(p + 1) * 8, :].transpose(1, 2, 0)),
                        "kT": np.ascontiguousarray(k[b, :, p, :].T),
                        "v": np.ascontiguousarray(v[b, :, p, :]),
                        "sinks": _rep(P["sinks"][p * 8:(p + 1) * 8])})
        nc = _prog(("swa", S), lambda: build_swa(S, 8))
    res = _run(nc, ims)
    o = np.empty((B, S, D), NPBF)
    for c in range(NCORES):
        b, p = c // 2, c % 2
        o[b, :, p * 512:(p + 1) * 512] = np.asarray(res[c]["o"])
    o = o.reshape(NTOK, D)
    if DEBUG is not None:
        DEBUG['o'] = o.copy(); DEBUG['qkv'] = qkv

    nc = _prog(("post", T, final), lambda: build_post(T, final))
    ims = []
    for c in range(NCORES):
        m = {"o": np.ascontiguousarray(o[c * T:(c + 1) * T]), "h": hs[c * T:(c + 1) * T], "g": _rep(P["mlp_g"]),
             "w_out": np.ascontiguousarray(P["w_out"], dtype=np.float32), "w_up": np.ascontiguousarray(P["w_up"], dtype=np.float32),
             "w_dn": np.ascontiguousarray(P["w_dn"], dtype=np.float32)}
        if final:
            m["gf"] = _rep(P["final_g"])
        ims.append(m)
    res = _run(nc, ims)
    return np.concatenate([np.asarray(r["hout"]) for r in res], 0)


def kernel(x, positions, attn_norm_g, mlp_norm_g, diff_w_in, diff_w_out, diff_lam_q1, diff_lam_k1,
           diff_lam_q2, diff_lam_k2, diff_subln_g, moba_w_in, moba_w_out, swa_w_in, swa_b_in,
           swa_sinks, swa_w_out, mlp_w_up, mlp_w_down, final_norm_g):
    x = np.asarray(x, np.float32)
    B, S, _ = x.shape
    depth = attn_norm_g.shape[0]
    hs = np.ascontiguousarray(x.reshape(B * S, D))
    pos = np.asarray(positions)
    for i in range(depth):
        mixer = i % 3
        slot = i // 3
        P = {"attn_g": attn_norm_g[i], "mlp_g": mlp_norm_g[i], "w_up": mlp_w_up[i], "w_dn": mlp_w_down[i], "final_g": final_norm_g}
        if mixer == 0:
            P.update(diff_w_in=diff_w_in[slot], w_out=diff_w_out[slot], lam_q1=diff_lam_q1[slot], lam_k1=diff_lam_k1[slot],
                     lam_q2=diff_lam_q2[slot], lam_k2=diff_lam_k2[slot], subln_g=diff_subln_g[slot],
                     lambda_init=0.8 - 0.6 * math.exp(-0.3 * i))
        elif mixer == 1:
            P.update(moba_w_in=moba_w_in[slot], w_out=moba_w_out[slot])
        else:
            P.update(swa_w_in=swa_w_in[slot], swa_b_in=swa_b_in[slot], sinks=np.asarray(swa_sinks[slot]), w_out=swa_w_out[slot])
        hs = run_layer(hs, pos, B, S, i, mixer, P, final=(i == depth - 1))
    return hs.reshape(B, S, D).astype(np.float32)
```

```python
import math
from contextlib import ExitStack
import numpy as np
import ml_dtypes
import concourse.bass as bass
import concourse.mybir as mybir
from concourse.bass_utils import run_bass_kernel_spmd

F32 = mybir.dt.float32
BF16 = mybir.dt.bfloat16
I32 = mybir.dt.int32
AF = mybir.ActivationFunctionType
ALU = mybir.AluOpType
AX = mybir.AxisListType
NPBF = ml_dtypes.bfloat16

D = 1024
DFF = 4096
HD = 64
EPS = 1e-6
THETA = 500000.0
NCORES = 8
BIG = 30000.0


class Buf:
    def __init__(self, ap=None):
        self.ap = ap
        self.w = None
        self.r = []


def _flat(xs):
    out = []
    for x in xs:
        if isinstance(x, (list, tuple)):
            out.extend(_flat(x))
        else:
            out.append(x)
    return out


class Eng:
    def __init__(self, name, eng, sem):
        self.name, self.eng, self.sem = name, eng, sem
        self.cnt = 0
        self.seen = {}


class KB:
    def __init__(self):
        self.nc = bass.Bass("TRN2", target_bir_lowering=False)
        self.es = ExitStack()
        nc = self.nc
        self.E = {}
        for nm, e in (("pe", nc.tensor), ("act", nc.scalar), ("dve", nc.vector), ("pool", nc.gpsimd), ("sp", nc.sync)):
            self.E[nm] = Eng(nm, e, self.es.enter_context(nc.semaphore("sem_" + nm)))
        self.dsem = [[self.es.enter_context(nc.semaphore("dsem%d" % i)), 0] for i in range(40)]
        self.dnext = 0
        self.uid = 0

    def sb(self, shape, dt, name=None):
        self.uid += 1
        t = self.es.enter_context(self.nc.sbuf_tensor((name or "sb") + "_s%d" % self.uid, list(shape), dt))
        return t

    def ps(self, shape, dt, name=None):
        self.uid += 1
        return self.es.enter_context(self.nc.psum_tensor((name or "ps") + "_p%d" % self.uid, list(shape), dt))

    def _sync(self, E, reads, writes):
        deps = []
        for b in reads:
            if b.w is not None:
                deps.append(b.w)
        for b in writes:
            if b.w is not None:
                deps.append(b.w)
            deps.extend(b.r)
        need = {}
        for (sem, val, owner) in deps:
            if owner is E:
                if E.name == "pe":
                    continue
                if val <= E.cnt - 2:
                    continue
            k = id(sem)
            if k not in need or need[k][1] < val:
                need[k] = (sem, val)
        for k, (sem, val) in need.items():
            if E.seen.get(k, 0) < val:
                E.eng.wait_ge(sem, val)
                E.seen[k] = val

    def op(self, en, fn, reads=(), writes=()):
        reads = _flat(reads); writes = _flat(writes)
        E = self.E[en]
        self._sync(E, reads, writes)
        ins = fn(E.eng)
        ins.then_inc(E.sem, 1)
        E.cnt += 1
        ev = (E.sem, E.cnt, E)
        for b in reads:
            b.r.append(ev)
        for b in writes:
            b.w = ev
            b.r = []
        return ev

    def mms(self, fns, reads=(), writes=()):
        reads = _flat(reads); writes = _flat(writes)
        E = self.E["pe"]
        self._sync(E, reads, writes)
        ins = None
        for f in fns:
            ins = f(E.eng)
        ins.then_inc(E.sem, 1)
        E.cnt += 1
        ev = (E.sem, E.cnt, E)
        for b in reads:
            b.r.append(ev)
        for b in writes:
            b.w = ev
            b.r = []
        return ev

    def dma(self, en, out, in_, reads=(), writes=()):
        reads = _flat(reads); writes = _flat(writes)
        E = self.E[en]
        self._sync(E, reads, writes)
        slot = self.dsem[self.dnext]
        self.dnext = (self.dnext + 1) % len(self.dsem)
        sem, cnt = slot
        k = id(sem)
        if cnt > 0 and E.seen.get(k, 0) < cnt:
            E.eng.wait_ge(sem, cnt)
            E.seen[k] = cnt
        E.eng.dma_start(out=out, in_=in_).then_inc(sem, 16)
        slot[1] = cnt + 16
        ev = (sem, cnt + 16, None)
        for b in reads:
            b.r.append(ev)
        for b in writes:
            b.w = ev
            b.r = []
        return ev

    def finish(self, bufs):
        E = self.E["sp"]
        self._sync(E, [], bufs)
        for nm, o in self.E.items():
            if o.cnt > 0 and E.seen.get(id(o.sem), 0) < o.cnt:
                E.eng.wait_ge(o.sem, o.cnt)
        for sem, cnt in self.dsem:
            if cnt > 0 and E.seen.get(id(sem), 0) < cnt:
                E.eng.wait_ge(sem, cnt)
        self.es.close()
        return self.nc


def make_identity(kb, dt=BF16):
    idn = kb.sb([128, 128], dt, "idn")
    b = Buf()
    kb.op("pool", lambda e: e.memset(idn[:, :], 0.0), writes=[b])
    kb.op("pool", lambda e: e.affine_select(out=idn[:, :], in_=idn[:, :], pattern=[[-1, 128]], compare_op=ALU.not_equal,
                                            fill=1.0, base=0, channel_multiplier=1), reads=[b], writes=[b])
    return idn, b


def rms_rstd(kb, x_ap, xb, sq, sqb, ss, ssb, n):
    kb.op("act", lambda e: e.activation(out=sq, in_=x_ap, func=AF.Square, accum_out=ss), reads=[xb], writes=[sqb, ssb])
    kb.op("dve", lambda e: e.tensor_scalar(out=ss, in0=ss, scalar1=1.0 / n, scalar2=EPS, op0=ALU.mult, op1=ALU.add),
          reads=[ssb], writes=[ssb])
    kb.op("act", lambda e: e.activation(out=ss, in_=ss, func=AF.Sqrt), reads=[ssb], writes=[ssb])
    kb.op("dve", lambda e: e.reciprocal(out=ss, in_=ss), reads=[ssb], writes=[ssb])


def build_pre(T, NCOL, n_rot_heads, has_bias):
    kb = KB()
    nc = kb.nc
    h = nc.dram_tensor("h", [T, D], F32, kind="ExternalInput").ap()
    pos = nc.dram_tensor("pos", [T, 1], I32, kind="ExternalInput").ap()
    g = nc.dram_tensor("g", [128, D], F32, kind="ExternalInput").ap()
    inv = nc.dram_tensor("inv", [128, 8], F32, kind="ExternalInput").ap()
    w = nc.dram_tensor("w", [D, NCOL], F32, kind="ExternalInput").ap()
    if has_bias:
        bias = nc.dram_tensor("bias", [128, NCOL], F32, kind="ExternalInput").ap()
    out = nc.dram_tensor("qkv", [T, NCOL], BF16, kind="ExternalOutput").ap()
    outb = Buf()

    idn, idnb = make_identity(kb)
    g_sb = kb.sb([128, D], F32, "g_sb"); gb = Buf()
    kb.dma("sp", g_sb[:, :], g[:, :], writes=[gb])
    inv_sb = kb.sb([128, 8], F32, "inv_sb"); invb = Buf()
    kb.dma("sp", inv_sb[:, :], inv[:, :], writes=[invb])
    if has_bias:
        b_sb = kb.sb([128, NCOL], F32, "b_sb"); bb = Buf()
        kb.dma("sp", b_sb[:, :], bias[:, :], writes=[bb])
    w_sb = kb.sb([128, 8, NCOL], BF16, "w_sb"); wb = [Buf() for _ in range(8)]
    wv = w.rearrange("(kc p) n -> p kc n", p=128)
    for kc in range(8):
        kb.dma("pool", w_sb[:, kc, :], wv[:, kc, :], writes=[wb[kc]])

    NB = 2
    h_sb = [kb.sb([128, D], F32, "h_sb%d" % i) for i in range(NB)]; hb = [Buf() for _ in range(NB)]
    sq = kb.sb([128, D], F32, "sq"); sqb = Buf()
    ss = kb.sb([128, 1], F32, "ss"); ssb = Buf()
    a_sb = kb.sb([128, D], BF16, "a_sb"); ab = Buf()
    aT = kb.sb([128, 8, 128], BF16, "aT"); aTb = Buf()
    tp = kb.ps([128, 8, 128], BF16, "tp"); tpb = Buf()
    NG = (NCOL + 511) // 512
    pp = [kb.ps([128, 512], F32, "pp%d" % i) for i in range(2)]; ppb = [Buf() for _ in range(2)]
    qkv = kb.sb([128, NCOL], F32, "qkv_sb"); qb = Buf()
    qo = [kb.sb([128, NCOL], BF16, "qo%d" % i) for i in range(2)]; qob = [Buf() for _ in range(2)]
    pos_i = kb.sb([128, 1], I32, "pos_i"); pib = Buf()
    pos_f = kb.sb([128, 1], F32, "pos_f"); pfb = Buf()
    ang = kb.sb([128, 8], F32, "ang"); angb = Buf()
    cs = kb.sb([128, 2, 8], F32, "cs"); csb = Buf()
    arg = kb.sb([128, 2, 8], F32, "arg"); argb = Buf()
    ki = kb.sb([128, 2, 8], I32, "ki"); kib = Buf()
    kf = kb.sb([128, 2, 8], F32, "kf"); kfb = Buf()
    nh = n_rot_heads
    tmp = kb.sb([128, 4, nh, 8], F32, "ropetmp"); tmpb = Buf()
    TWO_PI = 2.0 * math.pi

    nt = T // 128
    for t in range(nt):
        s = t % NB
        kb.dma("sp", h_sb[s][:, :], h[t * 128:(t + 1) * 128, :], writes=[hb[s]])
        kb.dma("sp", pos_i[:, :], pos[t * 128:(t + 1) * 128, :], writes=[pib])
        kb.op("dve", lambda e: e.tensor_copy(out=pos_f[:, :], in_=pos_i[:, :]), reads=[pib], writes=[pfb])
        kb.op("dve", lambda e: e.tensor_scalar(out=ang[:, :], in0=inv_sb[:, :], scalar1=pos_f[:, 0:1], scalar2=None, op0=ALU.mult),
              reads=[pfb, invb], writes=[angb])
        kb.op("dve", lambda e: e.tensor_copy(out=arg[:, 1, :], in_=ang[:, :]), reads=[angb], writes=[argb])
        kb.op("dve", lambda e: e.tensor_scalar(out=arg[:, 0, :], in0=ang[:, :], scalar1=0.5 * math.pi, scalar2=None, op0=ALU.add),
              reads=[angb], writes=[argb])
        kb.op("dve", lambda e: e.tensor_scalar(out=ki[:, :, :], in0=arg[:, :, :], scalar1=1.0 / TWO_PI, scalar2=None, op0=ALU.mult),
              reads=[argb], writes=[kib])
        kb.op("dve", lambda e: e.tensor_copy(out=kf[:, :, :], in_=ki[:, :, :]), reads=[kib], writes=[kfb])
        kb.op("dve", lambda e: e.scalar_tensor_tensor(out=cs[:, :, :], in0=kf[:, :, :], scalar=-TWO_PI, in1=arg[:, :, :], op0=ALU.mult, op1=ALU.add),
              reads=[kfb, argb], writes=[csb])
        kb.op("dve", lambda e: e.tensor_scalar(out=kf[:, :, :], in0=cs[:, :, :], scalar1=math.pi, scalar2=-TWO_PI, op0=ALU.is_gt, op1=ALU.mult),
              reads=[csb], writes=[kfb])
        kb.op("dve", lambda e: e.tensor_tensor(out=cs[:, :, :], in0=cs[:, :, :], in1=kf[:, :, :], op=ALU.add), reads=[csb, kfb], writes=[csb])
        kb.op("dve", lambda e: e.tensor_scalar(out=kf[:, :, :], in0=cs[:, :, :], scalar1=-math.pi, scalar2=TWO_PI, op0=ALU.is_lt, op1=ALU.mult),
              reads=[csb], writes=[kfb])
        kb.op("dve", lambda e: e.tensor_tensor(out=cs[:, :, :], in0=cs[:, :, :], in1=kf[:, :, :], op=ALU.add), reads=[csb, kfb], writes=[csb])
        kb.op("act", lambda e: e.activation(out=cs[:, :, :], in_=cs[:, :, :], func=AF.Sin), reads=[csb], writes=[csb])
        rms_rstd(kb, h_sb[s][:, :], hb[s], sq[:, :], sqb, ss[:, 0:1], ssb, D)
        kb.op("dve", lambda e: e.scalar_tensor_tensor(out=a_sb[:, :], in0=h_sb[s][:, :], scalar=ss[:, 0:1], in1=g_sb[:, :],
                                                      op0=ALU.mult, op1=ALU.mult), reads=[hb[s], ssb, gb], writes=[ab])
        kb.mms([(lambda e, kc=kc: e.transpose(out=tp[:, kc, :], in_=a_sb[:, kc * 128:(kc + 1) * 128], identity=idn[:, :]))
                for kc in range(8)], reads=[ab, idnb], writes=[tpb])
        kb.op("act", lambda e: e.activation(out=aT[:, :, :], in_=tp[:, :, :], func=AF.Copy), reads=[tpb], writes=[aTb])
        for n in range(NG):
            c0 = n * 512; c1 = min(NCOL, c0 + 512); cw = c1 - c0
            p = pp[n % 2]; pb = ppb[n % 2]
            kb.mms([(lambda e, kc=kc: e.matmul(p[:, 0:cw], lhsT=aT[:, kc, :], rhs=w_sb[:, kc, c0:c1], start=(kc == 0), stop=(kc == 7)))
                    for kc in range(8)], reads=[aTb, wb], writes=[pb])
            if has_bias:
                kb.op("dve", lambda e: e.tensor_tensor(out=qkv[:, c0:c1], in0=p[:, 0:cw], in1=b_sb[:, c0:c1], op=ALU.add),
                      reads=[pb, bb], writes=[qb])
            else:
                kb.op("dve" if n % 2 == 0 else "act",
                      (lambda e: e.tensor_copy(out=qkv[:, c0:c1], in_=p[:, 0:cw])) if n % 2 == 0 else
                      (lambda e: e.activation(out=qkv[:, c0:c1], in_=p[:, 0:cw], func=AF.Copy)),
                      reads=[pb], writes=[qb])
        xv = qkv[:, 0:nh * 64].rearrange("p (h d) -> p h d", d=64)
        x1 = xv[:, :, 0:8]; x2 = xv[:, :, 8:16]
        cb = cs[:, 0:1, :].to_broadcast([128, nh, 8]); sbb = cs[:, 1:2, :].to_broadcast([128, nh, 8])
        kb.op("dve", lambda e: e.tensor_tensor(out=tmp[:, 0, :, :], in0=x1, in1=cb, op=ALU.mult), reads=[qb, csb], writes=[tmpb])
        kb.op("dve", lambda e: e.tensor_tensor(out=tmp[:, 1, :, :], in0=x2, in1=sbb, op=ALU.mult), reads=[qb, csb], writes=[tmpb])
        kb.op("pool", lambda e: e.tensor_tensor(out=tmp[:, 2, :, :], in0=x2, in1=cb, op=ALU.mult), reads=[qb, csb], writes=[tmpb])
        kb.op("pool", lambda e: e.tensor_tensor(out=tmp[:, 3, :, :], in0=x1, in1=sbb, op=ALU.mult), reads=[qb, csb], writes=[tmpb])
        kb.op("dve", lambda e: e.tensor_tensor(out=x1, in0=tmp[:, 0, :, :], in1=tmp[:, 1, :, :], op=ALU.subtract), reads=[tmpb], writes=[qb])
        kb.op("dve", lambda e: e.tensor_tensor(out=x2, in0=tmp[:, 2, :, :], in1=tmp[:, 3, :, :], op=ALU.add), reads=[tmpb], writes=[qb])
        o = qo[t % 2]; ob = qob[t % 2]
        kb.op("act", lambda e: e.activation(out=o[:, :], in_=qkv[:, :], func=AF.Copy), reads=[qb], writes=[ob])
        kb.dma("sp", out[t * 128:(t + 1) * 128, :], o[:, :], reads=[ob])
    return kb.finish([outb])


def build_post(T, final):
    kb = KB()
    nc = kb.nc
    TG = 256
    NS = TG // 128
    o = nc.dram_tensor("o", [T, D], BF16, kind="ExternalInput").ap()
    h = nc.dram_tensor("h", [T, D], F32, kind="ExternalInput").ap()
    g = nc.dram_tensor("g", [128, D], F32, kind="ExternalInput").ap()
    w_out = nc.dram_tensor("w_out", [D, D], F32, kind="ExternalInput").ap()
    w_up = nc.dram_tensor("w_up", [D, DFF], F32, kind="ExternalInput").ap()
    w_dn = nc.dram_tensor("w_dn", [DFF, D], F32, kind="ExternalInput").ap()
    if final:
        gf = nc.dram_tensor("gf", [128, D], F32, kind="ExternalInput").ap()
    out = nc.dram_tensor("hout", [T, D], F32, kind="ExternalOutput").ap()
    outb = Buf()

    idn, idnb = make_identity(kb)
    g_sb = kb.sb([128, D], F32, "g_sb"); gb = Buf()
    kb.dma("sp", g_sb[:, :], g[:, :], writes=[gb])
    if final:
        gf_sb = kb.sb([128, D], F32, "gf_sb"); gfb = Buf()
        kb.dma("sp", gf_sb[:, :], gf[:, :], writes=[gfb])
    wo_sb = kb.sb([128, 8, D], BF16, "wo_sb"); wob = [Buf() for _ in range(8)]
    wu_sb = kb.sb([128, 8, DFF], BF16, "wu_sb"); wub = [Buf() for _ in range(8)]
    wd_sb = kb.sb([128, 32, D], BF16, "wd_sb"); wdb = [Buf() for _ in range(32)]
    wov = w_out.rearrange("(kc p) n -> p kc n", p=128)
    wuv = w_up.rearrange("(kc p) n -> p kc n", p=128)
    wdv = w_dn.rearrange("(kc p) n -> p kc n", p=128)
    for kc in range(8):
        kb.dma("pool", wo_sb[:, kc, :], wov[:, kc, :], writes=[wob[kc]])
    for kc in range(8):
        kb.dma("pool", wu_sb[:, kc, :], wuv[:, kc, :], writes=[wub[kc]])
    for kc in range(32):
        kb.dma("pool", wd_sb[:, kc, :], wdv[:, kc, :], writes=[wdb[kc]])

    h1 = kb.sb([128, NS, D], F32, "h1"); h1b = [Buf() for _ in range(NS)]
    ob16 = kb.sb([128, NS, D], BF16, "ob16"); obb = [Buf() for _ in range(NS)]
    xT = kb.sb([128, 8, TG], BF16, "xT"); xTb = Buf()
    hidT = kb.sb([128, 32, TG], BF16, "hidT"); hidb = Buf()
    sq = kb.sb([128, D], F32, "sq"); sqb = Buf()
    ss = kb.sb([128, 1], F32, "ss"); ssb = Buf()
    rl = [kb.sb([128, TG], F32, "rl%d" % i) for i in range(2)]; rlb = [Buf() for _ in range(2)]
    tp = kb.ps([128, 8, 128], BF16, "tp"); tpb = Buf()
    pp = [kb.ps([128, 512], F32, "pp%d" % i) for i in range(4)]; ppb = [Buf() for _ in range(4)]
    pi = [0]

    def nextp():
        i = pi[0] % 4
        pi[0] += 1
        return pp[i], ppb[i]

    def transposes(src_tile, srcb, si):
        kb.mms([(lambda e, kc=kc: e.transpose(out=tp[:, kc, :], in_=src_tile[:, kc * 128:(kc + 1) * 128], identity=idn[:, :]))
                for kc in range(8)], reads=[srcb, idnb], writes=[tpb])
        kb.op("act", lambda e: e.activation(out=xT[:, :, si * 128:(si + 1) * 128], in_=tp[:, :, :], func=AF.Copy),
              reads=[tpb], writes=[xTb])

    for tg in range(T // TG):
        r0 = tg * TG
        for si in range(NS):
            kb.dma("sp", ob16[:, si, :], o[r0 + si * 128:r0 + (si + 1) * 128, :], writes=[obb[si]])
            kb.dma("sp", h1[:, si, :], h[r0 + si * 128:r0 + (si + 1) * 128, :], writes=[h1b[si]])
        for si in range(NS):
            transposes(ob16[:, si, :], obb[si], si)
        for si in range(NS):
            for n in range(2):
                p, pb = nextp()
                kb.mms([(lambda e, kc=kc: e.matmul(p[:, :], lhsT=xT[:, kc, si * 128:(si + 1) * 128], rhs=wo_sb[:, kc, n * 512:(n + 1) * 512],
                                                   start=(kc == 0), stop=(kc == 7))) for kc in range(8)], reads=[xTb, wob], writes=[pb])
                kb.op("dve", lambda e: e.tensor_tensor(out=h1[:, si, n * 512:(n + 1) * 512], in0=p[:, :], in1=h1[:, si, n * 512:(n + 1) * 512],
                                                       op=ALU.add), reads=[pb, h1b[si]], writes=[h1b[si]])
        for si in range(NS):
            rms_rstd(kb, h1[:, si, :], h1b[si], sq[:, :], sqb, ss[:, 0:1], ssb, D)
            kb.op("dve", lambda e: e.scalar_tensor_tensor(out=ob16[:, si, :], in0=h1[:, si, :], scalar=ss[:, 0:1], in1=g_sb[:, :],
                                                          op0=ALU.mult, op1=ALU.mult), reads=[h1b[si], ssb, gb], writes=[obb[si]])
        for si in range(NS):
            transposes(ob16[:, si, :], obb[si], si)
        for fc in range(32):
            p, pb = nextp()
            kb.mms([(lambda e, kc=kc: e.matmul(p[:, 0:TG], lhsT=wu_sb[:, kc, fc * 128:(fc + 1) * 128], rhs=xT[:, kc, :],
                                               start=(kc == 0), stop=(kc == 7))) for kc in range(8)], reads=[xTb, wub], writes=[pb])
            r = rl[fc % 2]; rb = rlb[fc % 2]
            kb.op("act", lambda e: e.activation(out=r[:, :], in_=p[:, 0:TG], func=AF.Relu), reads=[pb], writes=[rb])
            kb.op("dve" if fc % 2 == 0 else "pool", lambda e: e.tensor_tensor(out=hidT[:, fc, :], in0=r[:, :], in1=r[:, :], op=ALU.mult),
                  reads=[rb], writes=[hidb])
        for si in range(NS):
            for n in range(2):
                p, pb = nextp()
                kb.mms([(lambda e, fc=fc: e.matmul(p[:, :], lhsT=hidT[:, fc, si * 128:(si + 1) * 128], rhs=wd_sb[:, fc, n * 512:(n + 1) * 512],
                                                   start=(fc == 0), stop=(fc == 31))) for fc in range(32)], reads=[hidb, wdb], writes=[pb])
                kb.op("dve", lambda e: e.tensor_tensor(out=h1[:, si, n * 512:(n + 1) * 512], in0=p[:, :], in1=h1[:, si, n * 512:(n + 1) * 512],
                                                       op=ALU.add), reads=[pb, h1b[si]], writes=[h1b[si]])
            if final:
                rms_rstd(kb, h1[:, si, :], h1b[si], sq[:, :], sqb, ss[:, 0:1], ssb, D)
                kb.op("dve", lambda e: e.scalar_tensor_tensor(out=h1[:, si, :], in0=h1[:, si, :], scalar=ss[:, 0:1], in1=gf_sb[:, :],
                                                              op0=ALU.mult, op1=ALU.mult), reads=[h1b[si], ssb, gfb], writes=[h1b[si]])
            kb.dma("sp", out[r0 + si * 128:r0 + (si + 1) * 128, :], h1[:, si, :], reads=[h1b[si]])
    return kb.finish([outb])


def make_masks(kb):
    cm = kb.sb([128, 128], BF16, "cmask"); cmb = Buf()
    am = kb.sb([128, 128], BF16, "amask"); amb = Buf()
    kb.op("pool", lambda e: e.memset(cm[:, :], 1.0), writes=[cmb])
    kb.op("pool", lambda e: e.affine_select(out=cm[:, :], in_=cm[:, :], pattern=[[1, 128]], compare_op=ALU.is_ge,
                                            fill=0.0, base=0, channel_multiplier=-1), reads=[cmb], writes=[cmb])
    kb.op("pool", lambda e: e.memset(am[:, :], 1.0), writes=[amb])
    kb.op("pool", lambda e: e.affine_select(out=am[:, :], in_=am[:, :], pattern=[[-1, 128]], compare_op=ALU.is_gt,
                                            fill=0.0, base=0, channel_multiplier=1), reads=[amb], writes=[amb])
    return cm, cmb, am, amb


class Pipe:
    def __init__(self):
        self.pending = None

    def push(self, st_fn, mid_fn, pv_fn, fin_fn=None):
        st_fn()
        self.flush()
        mid_fn()
        self.pending = (pv_fn, fin_fn)

    def flush(self):
        if self.pending is not None:
            pv, fin = self.pending
            self.pending = None
            pv()
            if fin is not None:
                fin()


def causal_attn_tile(kb, pipe, S, t, qT, qTb, kT, kTb, KC, V, Vb, DV, st, stb, pt, ptb, acc, accb, cm, cmb, cnt, fin):
    nq = min(4, S // 128 - 4 * t)
    last = 4 * t + nq - 1
    for j in range(last + 1):
        jj = j - 4 * t
        i0 = max(0, jj)
        qs = i0 * 128
        b = cnt[0] % len(st); cnt[0] += 1

        def st_fn(j=j, qs=qs, b=b):
            kb.mms([lambda e: e.matmul(st[b][:, qs:nq * 128], lhsT=kT[0:KC, j * 128:(j + 1) * 128], rhs=qT[0:KC, t * 512 + qs:t * 512 + nq * 128],
                                       start=True, stop=True)], reads=[qTb, kTb], writes=[stb[b]])

        def mid_fn(jj=jj, qs=qs, b=b):
            kb.op("act", lambda e: e.activation(out=pt[b][:, qs:nq * 128], in_=st[b][:, qs:nq * 128], func=AF.Exp, scale=0.125),
                  reads=[stb[b]], writes=[ptb[b]])
            if jj >= 0:
                kb.op("pool", lambda e: e.tensor_tensor(out=pt[b][:, qs:qs + 128], in0=pt[b][:, qs:qs + 128], in1=cm[:, :], op=ALU.mult),
                      reads=[ptb[b], cmb], writes=[ptb[b]])

        def pv_fn(j=j, i0=i0, b=b):
            for i in range(i0, nq):
                kb.mms([lambda e, i=i: e.matmul(acc[i][:, 0:DV + 1], lhsT=pt[b][:, i * 128:(i + 1) * 128], rhs=V[:, j, :],
                                                start=(j == 0), stop=(j == 4 * t + i))], reads=[ptb[b], Vb], writes=[accb[i]])

        pipe.push(st_fn, mid_fn, pv_fn, fin if j == last else None)
    return nq


def build_diff(S, NHL, lambda_init):
    kb = KB()
    nc = kb.nc
    qT_d = nc.dram_tensor("qT", [NHL * 2, 64, S], BF16, kind="ExternalInput").ap()
    kT_d = nc.dram_tensor("kT", [NHL * 2, 64, S], BF16, kind="ExternalInput").ap()
    v_d = nc.dram_tensor("v", [S, NHL * 128], BF16, kind="ExternalInput").ap()
    lam_d = nc.dram_tensor("lam", [128, 4, 64], F32, kind="ExternalInput").ap()
    sg_d = nc.dram_tensor("subg", [128, 128], F32, kind="ExternalInput").ap()
    out = nc.dram_tensor("o", [S, NHL * 128], BF16, kind="ExternalOutput").ap()
    outb = Buf()
    NT = S // 128
    cm, cmb, am, amb = make_masks(kb)
    lam_sb = kb.sb([128, 4, 64], F32, "lam_sb"); lb = Buf()
    kb.dma("sp", lam_sb[:, :, :], lam_d[:, :, :], writes=[lb])
    sg = kb.sb([128, 128], F32, "sg"); sgb = Buf()
    kb.dma("sp", sg[:, :], sg_d[:, :], writes=[sgb])
    lp = kb.sb([128, 2, 64], F32, "lp"); lpb = Buf()
    ls = kb.sb([128, 2], F32, "ls"); lsb = Buf()
    nlam = kb.sb([128, 1], F32, "nlam"); nlb = Buf()
    kb.op("dve", lambda e: e.tensor_tensor(out=lp[:, 0, :], in0=lam_sb[:, 0, :], in1=lam_sb[:, 1, :], op=ALU.mult), reads=[lb], writes=[lpb])
    kb.op("dve", lambda e: e.tensor_tensor(out=lp[:, 1, :], in0=lam_sb[:, 2, :], in1=lam_sb[:, 3, :], op=ALU.mult), reads=[lb], writes=[lpb])
    kb.op("dve", lambda e: e.tensor_reduce(out=ls[:, :], in_=lp[:, :, :], axis=AX.X, op=ALU.add), reads=[lpb], writes=[lsb])
    kb.op("act", lambda e: e.activation(out=ls[:, :], in_=ls[:, :], func=AF.Exp), reads=[lsb], writes=[lsb])
    kb.op("dve", lambda e: e.tensor_tensor(out=nlam[:, :], in0=ls[:, 1:2], in1=ls[:, 0:1], op=ALU.subtract), reads=[lsb], writes=[nlb])
    kb.op("dve", lambda e: e.tensor_scalar(out=nlam[:, :], in0=nlam[:, :], scalar1=-lambda_init, scalar2=None, op0=ALU.add),
          reads=[nlb], writes=[nlb])
    kb.op("dve", lambda e: e.tensor_scalar(out=sg[:, :], in0=sg[:, :], scalar1=1.0 - lambda_init, scalar2=None, op0=ALU.mult),
          reads=[sgb], writes=[sgb])

    qT = [kb.sb([64, S], BF16, "qT%d" % c) for c in range(2)]; qTb = [Buf() for _ in range(2)]
    kT = [kb.sb([64, S], BF16, "kT%d" % c) for c in range(2)]; kTb = [Buf() for _ in range(2)]
    V = kb.sb([128, NT, 129], BF16, "V"); Vb = [Buf() for _ in range((NT + 7) // 8)]
    st = [kb.ps([128, 512], F32, "st%d" % i) for i in range(2)]; stb = [Buf() for _ in range(2)]
    pt = [kb.sb([128, 512], BF16, "pt%d" % i) for i in range(2)]; ptb = [Buf() for _ in range(2)]
    acc = [kb.ps([128, 512], F32, "acc%d" % i) for i in range(4)]; accb = [Buf() for _ in range(4)]
    rz = kb.sb([128, 1], F32, "rz"); rzb = Buf()
    O0 = kb.sb([128, 4, 128], F32, "O0"); O0b = [Buf() for _ in range(4)]
    O1 = kb.sb([128, 128], F32, "O1"); O1b = Buf()
    sq = kb.sb([128, 128], F32, "sq"); sqb = Buf()
    ss = kb.sb([128, 1], F32, "ss"); ssb = Buf()
    ob = [kb.sb([128, 128], BF16, "ob%d" % i) for i in range(2)]; obb = [Buf() for _ in range(2)]
    cnt = [0]
    ocn = [0]
    pipe = Pipe()
    vv = v_d.rearrange("(t p) c -> p t c", p=128)
    for hd in range(NHL):
        kb.op("pool", lambda e: e.memset(V[:, :, 128:129], 1.0), writes=[Vb])
        for t0 in range(0, NT, 8):
            kb.dma("sp", V[:, t0:t0 + 8, 0:128], vv[:, t0:t0 + 8, hd * 128:(hd + 1) * 128], writes=[Vb[t0 // 8]])
        for c in range(2):
            kb.dma("sp", qT[c][:, :], qT_d[hd * 2 + c, :, :], writes=[qTb[c]])
            kb.dma("sp", kT[c][:, :], kT_d[hd * 2 + c, :, :], writes=[kTb[c]])
        for t in range((S + 511) // 512):
            for c in range(2):
                nq = min(4, S // 128 - 4 * t)

                def fin(t=t, c=c, hd=hd, nq=nq):
                    for i in range(nq):
                        kb.op("dve", lambda e: e.reciprocal(out=rz[:, :], in_=acc[i][:, 128:129]), reads=[accb[i]], writes=[rzb])
                        if c == 0:
                            kb.op("dve", lambda e: e.tensor_scalar(out=O0[:, i, :], in0=acc[i][:, 0:128], scalar1=rz[:, 0:1], scalar2=None, op0=ALU.mult),
                                  reads=[accb[i], rzb], writes=[O0b[i]])
                        else:
                            kb.op("dve", lambda e: e.tensor_scalar(out=O1[:, :], in0=acc[i][:, 0:128], scalar1=rz[:, 0:1], scalar2=None, op0=ALU.mult),
                                  reads=[accb[i], rzb], writes=[O1b])
                            kb.op("dve", lambda e: e.scalar_tensor_tensor(out=O1[:, :], in0=O1[:, :], scalar=nlam[:, 0:1], in1=O0[:, i, :],
                                                                          op0=ALU.mult, op1=ALU.add), reads=[O1b, nlb, O0b[i]], writes=[O1b])
                            rms_rstd(kb, O1[:, :], O1b, sq[:, :], sqb, ss[:, 0:1], ssb, 128)
                            k_ = ocn[0] % 2; ocn[0] += 1
                            o_ = ob[k_]; o_b = obb[k_]
                            kb.op("dve", lambda e: e.scalar_tensor_tensor(out=o_[:, :], in0=O1[:, :], scalar=ss[:, 0:1], in1=sg[:, :],
                                                                          op0=ALU.mult, op1=ALU.mult), reads=[O1b, ssb, sgb], writes=[o_b])
                            r0 = t * 512 + i * 128
                            kb.dma("sp", out[r0:r0 + 128, hd * 128:(hd + 1) * 128], o_[:, :], reads=[o_b])

                causal_attn_tile(kb, pipe, S, t, qT[c], qTb[c], kT[c], kTb[c], 64, V, Vb, 128, st, stb, pt, ptb, acc, accb, cm, cmb, cnt, fin)
        pipe.flush()
    return kb.finish([outb])


def build_moba(S, NHL):
    kb = KB()
    nc = kb.nc
    NB = S // 256
    NT = S // 128
    KC = 64 + 32
    qtok_d = nc.dram_tensor("qtok", [S, NHL * 64], BF16, kind="ExternalInput").ap()
    qT_d = nc.dram_tensor("qT", [NHL, 64, S], BF16, kind="ExternalInput").ap()
    kT_d = nc.dram_tensor("kT", [NHL, 64, S], BF16, kind="ExternalInput").ap()
    oh_d = nc.dram_tensor("onehot", [32, S], BF16, kind="ExternalInput").ap()
    v_d = nc.dram_tensor("v", [S, NHL * 64], BF16, kind="ExternalInput").ap()
    out = nc.dram_tensor("o", [S, NHL * 64], BF16, kind="ExternalOutput").ap()
    outb = Buf()
    cm, cmb, am, amb = make_masks(kb)
    idn, idnb = make_identity(kb)
    qa = kb.sb([KC, S], BF16, "qa"); qab = Buf()
    ka = kb.sb([KC, S], BF16, "ka"); kab = Buf()
    qTs = kb.sb([64, S], BF16, "qTs"); qTsb = Buf()
    V = kb.sb([128, NT, 65], BF16, "V"); Vb = [Buf() for _ in range((NT + 7) // 8)]
    km = kb.sb([64, 32], F32, "km"); kmb = Buf()
    kmh = kb.sb([64, 32], BF16, "kmh"); kmhb = Buf()
    qtk = kb.sb([128, NT, KC], BF16, "qtk"); qtkb = Buf()
    gp = kb.ps([128, 32], F32, "gp"); gpb = Buf()
    gs = kb.sb([128, 32], F32, "gs"); gsb = Buf()
    top8 = kb.sb([128, 8], F32, "top8"); t8b = Buf()
    thr = kb.sb([128, 1], F32, "thr"); thb = Buf()
    tpp = kb.ps([128, 128], BF16, "tpp"); tppb = Buf()
    st = [kb.ps([128, 512], F32, "st%d" % i) for i in range(2)]; stb = [Buf() for _ in range(2)]
    pt = [kb.sb([128, 512], BF16, "pt%d" % i) for i in range(2)]; ptb = [Buf() for _ in range(2)]
    acc = [kb.ps([128, 512], F32, "acc%d" % i) for i in range(4)]; accb = [Buf() for _ in range(4)]
    rz = kb.sb([128, 1], F32, "rz"); rzb = Buf()
    ob = [kb.sb([128, 64], BF16, "ob%d" % i) for i in range(2)]; obb = [Buf() for _ in range(2)]
    cnt = [0]
    ocn = [0]
    pipe = Pipe()
    vv = v_d.rearrange("(t p) c -> p t c", p=128)
    qtv = qtok_d.rearrange("(t p) c -> p t c", p=128)
    kb.dma("sp", ka[64:96, :], oh_d[:, :], writes=[kab])
    for hd in range(NHL):
        kb.op("pool", lambda e: e.memset(V[:, :, 64:65], 1.0), writes=[Vb])
        for t0 in range(0, NT, 8):
            kb.dma("sp", V[:, t0:t0 + 8, 0:64], vv[:, t0:t0 + 8, hd * 64:(hd + 1) * 64], writes=[Vb[t0 // 8]])
        kb.dma("sp", ka[0:64, :], kT_d[hd, :, :], writes=[kab])
        kb.dma("sp", qTs[:, :], qT_d[hd, :, :], writes=[qTsb])
        for t0 in range(0, NT, 8):
            kb.dma("sp", qtk[:, t0:t0 + 8, 0:64], qtv[:, t0:t0 + 8, hd * 64:(hd + 1) * 64], writes=[qtkb])
        kb.op("dve", lambda e: e.tensor_reduce(out=km[:, 0:NB], in_=ka[0:64, :].rearrange("p (n k) -> p n k", k=256), axis=AX.X, op=ALU.add),
              reads=[kab], writes=[kmb])
        kb.op("dve", lambda e: e.tensor_scalar(out=kmh[:, 0:NB], in0=km[:, 0:NB], scalar1=1.0 / 256, scalar2=None, op0=ALU.mult),
              reads=[kmb], writes=[kmhb])
        for qt in range(NT):
            own = qt // 2
            if own > 0:
                kb.mms([lambda e: e.matmul(gp[:, 0:own], lhsT=qTs[:, qt * 128:(qt + 1) * 128], rhs=kmh[:, 0:own], start=True, stop=True)],
                       reads=[qTsb, kmhb], writes=[gpb])
                kb.op("dve", lambda e: e.memset(gs[:, :], -1e30), writes=[gsb])
                kb.op("dve", lambda e: e.tensor_copy(out=gs[:, 0:own], in_=gp[:, 0:own]), reads=[gpb], writes=[gsb])
                kb.op("dve", lambda e: e.max(out=top8[:, :], in_=gs[:, :]), reads=[gsb], writes=[t8b])
                kb.op("dve", lambda e: e.tensor_scalar(out=thr[:, :], in0=top8[:, 2:3], scalar1=-1e29, scalar2=None, op0=ALU.max),
                      reads=[t8b], writes=[thb])
                kb.op("dve", lambda e: e.tensor_scalar(out=gs[:, :], in0=gs[:, :], scalar1=thr[:, 0:1], scalar2=-1.0, op0=ALU.is_ge, op1=ALU.add),
                      reads=[gsb, thb], writes=[gsb])
                kb.op("dve", lambda e: e.tensor_scalar(out=qtk[:, qt, 64:96], in0=gs[:, :], scalar1=BIG, scalar2=None, op0=ALU.mult),
                      reads=[gsb], writes=[qtkb])
            else:
                kb.op("dve", lambda e: e.memset(qtk[:, qt, 64:96], -BIG), writes=[qtkb])
            kb.op("dve", lambda e: e.memset(qtk[:, qt, 64 + own:65 + own], 0.0), writes=[qtkb])
            kb.mms([lambda e: e.transpose(out=tpp[0:KC, :], in_=qtk[:, qt, :], identity=idn[:, :])], reads=[qtkb, idnb], writes=[tppb])
            kb.op("act", lambda e: e.activation(out=qa[:, qt * 128:(qt + 1) * 128], in_=tpp[0:KC, :], func=AF.Copy), reads=[tppb], writes=[qab])
        for t in range((S + 511) // 512):
            nq = min(4, S // 128 - 4 * t)

            def fin(t=t, hd=hd, nq=nq):
                for i in range(nq):
                    kb.op("dve", lambda e: e.reciprocal(out=rz[:, :], in_=acc[i][:, 64:65]), reads=[accb[i]], writes=[rzb])
                    k_ = ocn[0] % 2; ocn[0] += 1
                    o_ = ob[k_]; o_b = obb[k_]
                    kb.op("dve", lambda e: e.tensor_scalar(out=o_[:, :], in0=acc[i][:, 0:64], scalar1=rz[:, 0:1], scalar2=None, op0=ALU.mult),
                          reads=[accb[i], rzb], writes=[o_b])
                    r0 = t * 512 + i * 128
                    kb.dma("sp", out[r0:r0 + 128, hd * 64:(hd + 1) * 64], o_[:, :], reads=[o_b])

            causal_attn_tile(kb, pipe, S, t, qa, qab, ka, kab, KC, V, Vb, 64, st, stb, pt, ptb, acc, accb, cm, cmb, cnt, fin)
        pipe.flush()
    return kb.finish([outb])


def build_swa(S, NQ):
    kb = KB()
    nc = kb.nc
    NT = S // 128
    qT_d = nc.dram_tensor("qT", [NQ, 64, S], BF16, kind="ExternalInput").ap()
    kT_d = nc.dram_tensor("kT", [64, S], BF16, kind="ExternalInput").ap()
    v_d = nc.dram_tensor("v", [S, 64], BF16, kind="ExternalInput").ap()
    sk_d = nc.dram_tensor("sinks", [128, NQ], F32, kind="ExternalInput").ap()
    out = nc.dram_tensor("o", [S, NQ * 64], BF16, kind="ExternalOutput").ap()
    outb = Buf()
    cm, cmb, am, amb = make_masks(kb)
    esk = kb.sb([128, NQ], F32, "esk"); eskb = Buf()
    kb.dma("sp", esk[:, :], sk_d[:, :], writes=[eskb])
    kb.op("act", lambda e: e.activation(out=esk[:, :], in_=esk[:, :], func=AF.Exp), reads=[eskb], writes=[eskb])
    kT = kb.sb([64, S], BF16, "kT"); kTb = Buf()
    kb.dma("sp", kT[:, :], kT_d[:, :], writes=[kTb])
    V = kb.sb([128, NT, 65], BF16, "V"); Vb = [Buf() for _ in range((NT + 7) // 8)]
    kb.op("pool", lambda e: e.memset(V[:, :, 64:65], 1.0), writes=[Vb])
    vv = v_d.rearrange("(t p) c -> p t c", p=128)
    for t0 in range(0, NT, 8):
        kb.dma("sp", V[:, t0:t0 + 8, 0:64], vv[:, t0:t0 + 8, :], writes=[Vb[t0 // 8]])
    qT = [kb.sb([64, S], BF16, "qT%d" % i) for i in range(2)]; qTb = [Buf() for _ in range(2)]
    st = [kb.ps([128, 512], F32, "st%d" % i) for i in range(2)]; stb = [Buf() for _ in range(2)]
    pt = [kb.sb([128, 256], BF16, "pt%d" % i) for i in range(2)]; ptb = [Buf() for _ in range(2)]
    acc = [kb.ps([128, 512], F32, "acc%d" % i) for i in range(2)]; accb = [Buf() for _ in range(2)]
    rz = kb.sb([128, 1], F32, "rz"); rzb = Buf()
    ob = [kb.sb([128, 64], BF16, "ob%d" % i) for i in range(2)]; obb = [Buf() for _ in range(2)]
    u = 0
    for g in range(NQ):
        q = qT[g % 2]; qb = qTb[g % 2]
        kb.dma("sp", q[:, :], qT_d[g, :, :], writes=[qb])
        for n in range(NT):
            b = u % 2; u += 1
            js = [n - 1, n] if n > 0 else [n]
            kb.mms([(lambda e, ji=ji, j=j: e.matmul(st[b][:, ji * 128:(ji + 1) * 128], lhsT=kT[:, j * 128:(j + 1) * 128], rhs=q[:, n * 128:(n + 1) * 128],
                                                    start=True, stop=True)) for ji, j in enumerate(js)], reads=[qb, kTb], writes=[stb[b]])
            w = len(js) * 128
            kb.op("act", lambda e: e.activation(out=pt[b][:, 0:w], in_=st[b][:, 0:w], func=AF.Exp, scale=0.125), reads=[stb[b]], writes=[ptb[b]])
            for ji, j in enumerate(js):
                m = cm if j == n else am
                mb = cmb if j == n else amb
                kb.op("pool" if ji == 0 else "dve", lambda e, ji=ji, m=m: e.tensor_tensor(out=pt[b][:, ji * 128:(ji + 1) * 128], in0=pt[b][:, ji * 128:(ji + 1) * 128],
                                                                                   in1=m[:, :], op=ALU.mult), reads=[ptb[b], mb], writes=[ptb[b]])
            kb.mms([(lambda e, ji=ji, j=j: e.matmul(acc[b][:, 0:65], lhsT=pt[b][:, ji * 128:(ji + 1) * 128], rhs=V[:, j, :],
                                                    start=(ji == 0), stop=(ji == len(js) - 1))) for ji, j in enumerate(js)],
                   reads=[ptb[b], Vb], writes=[accb[b]])
            kb.op("dve", lambda e: e.tensor_tensor(out=rz[:, :], in0=acc[b][:, 64:65], in1=esk[:, g:g + 1], op=ALU.add), reads=[accb[b], eskb], writes=[rzb])
            kb.op("dve", lambda e: e.reciprocal(out=rz[:, :], in_=rz[:, :]), reads=[rzb], writes=[rzb])
            o_ = ob[b]; o_b = obb[b]
            kb.op("dve", lambda e: e.tensor_scalar(out=o_[:, :], in0=acc[b][:, 0:64], scalar1=rz[:, 0:1], scalar2=None, op0=ALU.mult),
                  reads=[accb[b], rzb], writes=[o_b])
            kb.dma("sp", out[n * 128:(n + 1) * 128, g * 64:(g + 1) * 64], o_[:, :], reads=[o_b])
    return kb.finish([outb])


_CACHE = {}
DEBUG = None


def _prog(key, fn):
    if key not in _CACHE:
        _CACHE[key] = fn()
    return _CACHE[key]


def _run(nc, in_maps):
    res = run_bass_kernel_spmd(nc, in_maps, core_ids=list(range(NCORES)))
    return res.results


def _rep(v, n=128):
    v = np.asarray(v, np.float32).reshape(1, -1)
    return np.ascontiguousarray(np.broadcast_to(v, (n, v.shape[1])))


def _inv_table():
    inv = np.float32(THETA) ** (-(np.arange(0, 16, 2, dtype=np.float32) / np.float32(16)))
    return _rep(inv.astype(np.float32))


def run_layer(hs, pos, B, S, layer, mixer, P, final):
    NTOK = B * S
    T = NTOK // NCORES
    inv = _inv_table()
    posf = np.ascontiguousarray(pos.reshape(NTOK, 1).astype(np.int32))
    if mixer == 0:
        w_in = P["diff_w_in"]; ncol = 3072; nrot = 32; bias = None
    elif mixer == 1:
        w_in = P["moba_w_in"]; ncol = 3072; nrot = 32; bias = None
    else:
        w_in = P["swa_w_in"]; ncol = 1280; nrot = 18; bias = P["swa_b_in"]
    nc = _prog(("pre", T, ncol, nrot, bias is not None), lambda: build_pre(T, ncol, nrot, bias is not None))
    ims = []
    for c in range(NCORES):
        m = {"h": hs[c * T:(c + 1) * T], "pos": posf[c * T:(c + 1) * T], "g": _rep(P["attn_g"]), "inv": inv,
             "w": np.ascontiguousarray(w_in, dtype=np.float32)}
        if bias is not None:
            m["bias"] = _rep(bias)
        ims.append(m)
    res = _run(nc, ims)
    qkv = np.concatenate([np.asarray(r["qkv"]) for r in res], 0).reshape(B, S, ncol)

    assert B * 2 == NCORES
    ims = []
    if mixer == 0:
        q = qkv[:, :, 0:1024].reshape(B, S, 16, 64); k = qkv[:, :, 1024:2048].reshape(B, S, 16, 64)
        v = qkv[:, :, 2048:3072]
        lam = np.stack([_rep(P["lam_q1"]), _rep(P["lam_k1"]), _rep(P["lam_q2"]), _rep(P["lam_k2"])], 1)
        for c in range(NCORES):
            b, p = c // 2, c % 2
            ims.append({"qT": np.ascontiguousarray(q[b, :, p * 8:(p + 1) * 8, :].transpose(1, 2, 0)),
                        "kT": np.ascontiguousarray(k[b, :, p * 8:(p + 1) * 8, :].transpose(1, 2, 0)),
                        "v": np.ascontiguousarray(v[b, :, p * 512:(p + 1) * 512]),
                        "lam": np.ascontiguousarray(lam), "subg": _rep(P["subln_g"])})
        nc = _prog(("diff", S, P["lambda_init"]), lambda: build_diff(S, 4, P["lambda_init"]))
    elif mixer == 1:
        q = qkv[:, :, 0:1024].reshape(B, S, 16, 64); k = qkv[:, :, 1024:2048].reshape(B, S, 16, 64)
        v = qkv[:, :, 2048:3072]
        oh = np.zeros((32, S), NPBF)
        for n in range(S // 256):
            oh[n, n * 256:(n + 1) * 256] = 1
        for c in range(NCORES):
            b, p = c // 2, c % 2
            ims.append({"qtok": np.ascontiguousarray(qkv[b, :, p * 512:(p + 1) * 512]),
                        "qT": np.ascontiguousarray(q[b, :, p * 8:(p + 1) * 8, :].transpose(1, 2, 0)),
                        "kT": np.ascontiguousarray(k[b, :, p * 8:(p + 1) * 8, :].transpose(1, 2, 0)),
                        "onehot": oh,
                        "v": np.ascontiguousarray(v[b, :, p * 512:(p + 1) * 512])})
        nc = _prog(("moba", S), lambda: build_moba(S, 8))
    else:
        q = qkv[:, :, 0:1024].reshape(B, S, 16, 64)
        k = qkv[:, :, 1024:1152].reshape(B, S, 2, 64); v = qkv[:, :, 1152:1280].reshape(B, S, 2, 64)
        for c in range(NCORES):
            b, p = c // 2, c % 2
            ims.append({"qT": np.ascontiguousarray(q[b, :, p * 8:(p + 1) * 8, :].transpose(1, 2, 0)),
                        "kT": np.ascontiguousarray(k[b, :, p, :].T),
                        "v": np.ascontiguousarray(v[b, :, p, :]),
                        "sinks": _rep(P["sinks"][p * 8:(p + 1) * 8])})
        nc = _prog(("swa", S), lambda: build_swa(S, 8))
    res = _run(nc, ims)
    o = np.empty((B, S, D), NPBF)
    for c in range(NCORES):
        b, p = c // 2, c % 2
        o[b, :, p * 512:(p + 1) * 512] = np.asarray(res[c]["o"])
    o = o.reshape(NTOK, D)
    if DEBUG is not None:
        DEBUG['o'] = o.copy(); DEBUG['qkv'] = qkv

    nc = _prog(("post", T, final), lambda: build_post(T, final))
    ims = []
    for c in range(NCORES):
        m = {"o": np.ascontiguousarray(o[c * T:(c + 1) * T]), "h": hs[c * T:(c + 1) * T], "g": _rep(P["mlp_g"]),
             "w_out": np.ascontiguousarray(P["w_out"], dtype=np.float32), "w_up": np.ascontiguousarray(P["w_up"], dtype=np.float32),
             "w_dn": np.ascontiguousarray(P["w_dn"], dtype=np.float32)}
        if final:
            m["gf"] = _rep(P["final_g"])
        ims.append(m)
    res = _run(nc, ims)
    return np.concatenate([np.asarray(r["hout"]) for r in res], 0)


def kernel(x, positions, attn_norm_g, mlp_norm_g, diff_w_in, diff_w_out, diff_lam_q1, diff_lam_k1,
           diff_lam_q2, diff_lam_k2, diff_subln_g, moba_w_in, moba_w_out, swa_w_in, swa_b_in,
           swa_sinks, swa_w_out, mlp_w_up, mlp_w_down, final_norm_g):
    x = np.asarray(x, np.float32)
    B, S, _ = x.shape
    depth = attn_norm_g.shape[0]
    hs = np.ascontiguousarray(x.reshape(B * S, D))
    pos = np.asarray(positions)
    for i in range(depth):
        mixer = i % 3
        slot = i // 3
        P = {"attn_g": attn_norm_g[i], "mlp_g": mlp_norm_g[i], "w_up": mlp_w_up[i], "w_dn": mlp_w_down[i], "final_g": final_norm_g}
        if mixer == 0:
            P.update(diff_w_in=diff_w_in[slot], w_out=diff_w_out[slot], lam_q1=diff_lam_q1[slot], lam_k1=diff_lam_k1[slot],
                     lam_q2=diff_lam_q2[slot], lam_k2=diff_lam_k2[slot], subln_g=diff_subln_g[slot],
                     lambda_init=0.8 - 0.6 * math.exp(-0.3 * i))
        elif mixer == 1:
            P.update(moba_w_in=moba_w_in[slot], w_out=moba_w_out[slot])
        else:
            P.update(swa_w_in=swa_w_in[slot], swa_b_in=swa_b_in[slot], sinks=np.asarray(swa_sinks[slot]), w_out=swa_w_out[slot])
        hs = run_layer(hs, pos, B, S, i, mixer, P, final=(i == depth - 1))
    return hs.reshape(B, S, D).astype(np.float32)
```
